# Optimizing a Trainium2 kernel written in Bass

```python
import math
import jax, jax.numpy as jnp
from jax import lax
import numpy as np

D_MODEL = 1024
BATCH = 8
SEQ = 4096
DEPTH = 4

HEAD_DIM = 64
DA_HEADS = 8
DA_V_DIM = 2 * HEAD_DIM
SW_Q_HEADS = 16
SW_KV_HEADS = 2
SW_GROUP = SW_Q_HEADS // SW_KV_HEADS
WINDOW = 128
BLOCK = 128
D_FF = 2816
CONV_WIDTH = 3
NUM_BUCKETS = 32
MAX_EXACT = NUM_BUCKETS // 2
MAX_DISTANCE = 128
N_BIAS_HEADS = DA_HEADS + SW_Q_HEADS
EPS = 1e-6

DA_Q = DA_HEADS * 2 * HEAD_DIM
DA_K = DA_HEADS * 2 * HEAD_DIM
DA_V = DA_HEADS * DA_V_DIM
SW_Q = SW_Q_HEADS * HEAD_DIM
SW_K = SW_KV_HEADS * HEAD_DIM
SW_V = SW_KV_HEADS * HEAD_DIM
GATE_COLS = 2 * D_MODEL
IN_COLS = DA_Q + DA_K + DA_V + SW_Q + SW_K + SW_V + GATE_COLS
SPLITS = list(np.cumsum([DA_Q, DA_K, DA_V, SW_Q, SW_K, SW_V, D_MODEL])[:].tolist())

kernel_name = "hybrid_diffattn_swa_sink_convglu_adaln"


def rmsnorm(x, g):
    xf = x.astype(jnp.float32)
    y = xf * lax.rsqrt(jnp.mean(xf * xf, axis=-1, keepdims=True) + EPS)
    return y.astype(x.dtype) * g


def modulate(h, shift, scale):
    return h * (1 + scale[:, None, :]) + shift[:, None, :]


def rel_bucket(dist):
    n = jnp.maximum(dist, 0)
    large = MAX_EXACT + (jnp.log(jnp.maximum(n, 1).astype(jnp.float32) / MAX_EXACT)
                         / math.log(MAX_DISTANCE / MAX_EXACT)
                         * (NUM_BUCKETS - MAX_EXACT)).astype(jnp.int32)
    large = jnp.minimum(large, NUM_BUCKETS - 1)
    return jnp.where(n < MAX_EXACT, n, large)


def diff_attention(q, k, v, lam, lam_init, subln_g, bias_table):
    B, S = q.shape[0], q.shape[1]
    nblk = S // BLOCK
    scale = HEAD_DIM ** -0.5
    qb = q.reshape(B, nblk, BLOCK, DA_HEADS, 2, HEAD_DIM).swapaxes(0, 1)
    k_pos = jnp.arange(S)

    def one_block(args):
        q_blk, i = args
        q_pos = i * BLOCK + jnp.arange(BLOCK)
        dist = q_pos[:, None] - k_pos[None, :]
        bias = jnp.transpose(bias_table[rel_bucket(dist)], (2, 0, 1))[:, None]
        s = jnp.einsum('bqhmd,bkhmd->bhmqk', q_blk, k).astype(jnp.float32) * scale + bias
        s = jnp.where(dist >= 0, s, -jnp.inf)
        p = jax.nn.softmax(s, axis=-1)
        a = p[:, :, 0] - lam * p[:, :, 1]
        return jnp.einsum('bhqk,bkhe->bqhe', a.astype(v.dtype), v)

    out = lax.map(one_block, (qb, jnp.arange(nblk)))
    out = out.swapaxes(0, 1).reshape(B, S, DA_HEADS, DA_V_DIM)
    out = rmsnorm(out, subln_g) * (1 - lam_init)
    return out.reshape(B, S, DA_V)


def sliding_window_attention(q, k, v, sinks, bias_table):
    B, S = q.shape[0], q.shape[1]
    nblk = S // BLOCK
    scale = HEAD_DIM ** -0.5
    qb = q.reshape(B, nblk, BLOCK, SW_KV_HEADS, SW_GROUP, HEAD_DIM).swapaxes(0, 1)

    def band(t):
        tb = t.reshape(B, nblk, BLOCK, SW_KV_HEADS, HEAD_DIM)
        prev = jnp.pad(tb[:, :-1], ((0, 0), (1, 0), (0, 0), (0, 0), (0, 0)))
        return jnp.concatenate([prev, tb], axis=2).swapaxes(0, 1)

    kb, vb = band(k), band(v)
    dist = (jnp.arange(BLOCK)[:, None] + BLOCK) - jnp.arange(2 * BLOCK)[None, :]
    in_window = (dist >= 0) & (dist < WINDOW)
    k_valid = (jnp.arange(nblk)[:, None] * BLOCK - BLOCK + jnp.arange(2 * BLOCK)[None, :]) >= 0
    bias = bias_table[rel_bucket(dist)].reshape(BLOCK, 2 * BLOCK, SW_KV_HEADS, SW_GROUP)
    bias = jnp.transpose(bias, (2, 3, 0, 1))
    sink = sinks.reshape(SW_KV_HEADS, SW_GROUP).astype(jnp.float32)[:, :, None, None]

    def one_block(args):
        q_blk, k_blk, v_blk, valid = args
        s = jnp.einsum('bqhgd,bkhd->bhgqk', q_blk, k_blk).astype(jnp.float32) * scale + bias
        mask = in_window & valid[None, :]
        s = jnp.where(mask, s, -jnp.inf)
        m = jnp.maximum(jnp.max(s, axis=-1, keepdims=True), sink)
        e = jnp.exp(s - m)
        p = e / (jnp.sum(e, axis=-1, keepdims=True) + jnp.exp(sink - m))
        return jnp.einsum('bhgqk,bkhd->bqhgd', p.astype(v_blk.dtype), v_blk)

    out = lax.map(one_block, (qb, kb, vb, k_valid))
    return out.swapaxes(0, 1).reshape(B, S, SW_Q)


def conv_glu_ffn(h, w_in, conv_w, conv_b, w_out):
    a, b = jnp.split(h @ w_in, 2, axis=-1)
    S = a.shape[1]
    a_pad = jnp.pad(a, ((0, 0), (CONV_WIDTH - 1, 0), (0, 0)))
    a = conv_b + sum(a_pad[:, j:j + S] * conv_w[j] for j in range(CONV_WIDTH))
    return (jax.nn.silu(a) * b) @ w_out


def setup_inputs(seed: int = 0) -> dict:
    key = jax.random.key(seed)
    ks = jax.random.split(key, 24)
    f32 = jnp.float32
    nrm = lambda k, shape, s: jax.random.normal(k, shape, f32) * s
    L, D = DEPTH, D_MODEL
    return {
        "x": nrm(ks[0], (BATCH, SEQ, D), 1.0),
        "c": nrm(ks[1], (BATCH, D), 1.0),
        "rel_bias": nrm(ks[2], (NUM_BUCKETS, N_BIAS_HEADS), 0.5),
        "ada_w": nrm(ks[3], (L, D, 6 * D), 0.5 * D ** -0.5),
        "ada_b": nrm(ks[4], (L, 6 * D), 0.02),
        "norm_mix_g": 1.0 + nrm(ks[5], (L, D), 0.05),
        "norm_ffn_g": 1.0 + nrm(ks[6], (L, D), 0.05),
        "w_in": nrm(ks[7], (L, D, IN_COLS), D ** -0.5),
        "lam_q1": nrm(ks[8], (L, HEAD_DIM), 0.1),
        "lam_k1": nrm(ks[9], (L, HEAD_DIM), 0.1),
        "lam_q2": nrm(ks[10], (L, HEAD_DIM), 0.1),
        "lam_k2": nrm(ks[11], (L, HEAD_DIM), 0.1),
        "subln_g": 1.0 + nrm(ks[12], (L, DA_V_DIM), 0.05),
        "sinks": nrm(ks[13], (L, SW_Q_HEADS), 0.5),
        "w_pa": nrm(ks[14], (L, DA_V, D), DA_V ** -0.5),
        "w_pb": nrm(ks[15], (L, SW_Q, D), SW_Q ** -0.5),
        "w_o": nrm(ks[16], (L, D, D), D ** -0.5),
        "w_ffn_in": nrm(ks[17], (L, D, 2 * D_FF), D ** -0.5),
        "conv_w": nrm(ks[18], (L, CONV_WIDTH, D_FF), CONV_WIDTH ** -0.5),
        "conv_b": nrm(ks[19], (L, D_FF), 0.01),
        "w_ffn_out": nrm(ks[20], (L, D_FF, D), D_FF ** -0.5),
        "final_g": 1.0 + nrm(ks[21], (D,), 0.05),
    }


def reference(x, c, rel_bias, ada_w, ada_b, norm_mix_g, norm_ffn_g, w_in,
              lam_q1, lam_k1, lam_q2, lam_k2, subln_g, sinks, w_pa, w_pb, w_o,
              w_ffn_in, conv_w, conv_b, w_ffn_out, final_g):
    B, S = x.shape[0], x.shape[1]
    bias_da = rel_bias[:, :DA_HEADS]
    bias_sw = rel_bias[:, DA_HEADS:]
    c_act = jax.nn.silu(c)
    for l in range(DEPTH):
        mod = c_act @ ada_w[l] + ada_b[l]
        sh1, sc1, g1, sh2, sc2, g2 = jnp.split(mod, 6, axis=-1)

        h = modulate(rmsnorm(x, norm_mix_g[l]), sh1, sc1)
        proj = h @ w_in[l]
        qa, ka, va, qs, ksw, vs, ga, gb = jnp.split(proj, SPLITS, axis=-1)
        lam_init = 0.8 - 0.6 * math.exp(-0.3 * l)
        lam = (jnp.exp(jnp.sum(lam_q1[l] * lam_k1[l])) - jnp.exp(jnp.sum(lam_q2[l] * lam_k2[l]))
               + lam_init)
        ya = diff_attention(qa.reshape(B, S, DA_HEADS, 2, HEAD_DIM),
                            ka.reshape(B, S, DA_HEADS, 2, HEAD_DIM),
                            va.reshape(B, S, DA_HEADS, DA_V_DIM),
                            lam, lam_init, subln_g[l], bias_da)
        yb = sliding_window_attention(qs.reshape(B, S, SW_KV_HEADS, SW_GROUP, HEAD_DIM),
                                      ksw.reshape(B, S, SW_KV_HEADS, HEAD_DIM),
                                      vs.reshape(B, S, SW_KV_HEADS, HEAD_DIM),
                                      sinks[l], bias_sw)
        merged = jax.nn.sigmoid(ga) * (ya @ w_pa[l]) + jax.nn.sigmoid(gb) * (yb @ w_pb[l])
        x = x + g1[:, None, :] * (merged @ w_o[l])

        h = modulate(rmsnorm(x, norm_ffn_g[l]), sh2, sc2)
        x = x + g2[:, None, :] * conv_glu_ffn(h, w_ffn_in[l], conv_w[l], conv_b[l], w_ffn_out[l])
    return rmsnorm(x, final_g)
```

```python
import math
import bisect
import numpy as np
import concourse.bass as bass
import concourse.mybir as mybir
from concourse.bass_utils import run_bass_kernel_spmd

F32 = mybir.dt.float32
BF16 = mybir.dt.bfloat16
AF = mybir.ActivationFunctionType
ALU = mybir.AluOpType

D = 1024
DEPTH = 4
HD = 64
NB_BUCKETS = 32
D_FF = 2816
NFF = D_FF // 128
IN_COLS = 6400
EPS = 1e-6
GT = 512
MASKV = -30000.0
SCALE = HD ** -0.5

C_QA, C_KA, C_VA, C_QS, C_KS, C_VS, C_GA, C_GB = 0, 1024, 2048, 3072, 4096, 4224, 4352, 5376


class _Op:
    __slots__ = ("eng", "fn", "deps", "key", "count", "signal", "idx", "waits")


class Prog:
    def __init__(self):
        self.ops = []
        self.lastw = {}
        self.readers = {}

    def op(self, eng, fn, reads=(), writes=(), key=None):
        idx = len(self.ops)
        deps = set()
        for r in reads:
            w = self.lastw.get(r)
            if w is not None:
                deps.add(w)
        for w_ in writes:
            w = self.lastw.get(w_)
            if w is not None:
                deps.add(w)
            rl = self.readers.get(w_)
            if rl:
                deps.update(rl)
        for r in reads:
            self.readers.setdefault(r, []).append(idx)
        for w_ in writes:
            self.lastw[w_] = idx
            self.readers[w_] = []
        o = _Op()
        o.eng, o.fn, o.deps, o.key, o.idx = eng, fn, deps, key, idx
        o.count, o.signal, o.waits = 0, False, None
        self.ops.append(o)
        return idx

    def final_wait(self, eng):
        last = {}
        for o in self.ops:
            if o.fn is None:
                continue
            last[("k", o.key) if o.key is not None else o.eng] = o.idx
        idx = len(self.ops)
        o = _Op()
        o.eng, o.fn, o.deps, o.key, o.idx = eng, None, set(last.values()), None, idx
        o.count, o.signal, o.waits = 0, False, None
        self.ops.append(o)

    def finalize(self):
        ops = self.ops
        for o in ops:
            for d in o.deps:
                od = ops[d]
                if od.key is not None:
                    continue
                if od.eng == "pe" and o.eng == "pe" and o.key is None:
                    continue
                od.signal = True
        cnt = {}
        self.keylist = {}
        for o in ops:
            if o.key is not None:
                c = cnt.get(("k", o.key), 0) + 16
                cnt[("k", o.key)] = c
                o.count = c
                self.keylist.setdefault(o.key, ([], []))
                self.keylist[o.key][0].append(o.idx)
                self.keylist[o.key][1].append(c)
            elif o.signal:
                c = cnt.get(o.eng, 0) + 1
                cnt[o.eng] = c
                o.count = c
        waited = {}
        for o in ops:
            need = {}
            for d in o.deps:
                od = ops[d]
                if od.key is not None:
                    idxs, cums = self.keylist[od.key]
                    p = bisect.bisect_left(idxs, o.idx) - 1
                    sem = ("k", od.key)
                    val = cums[p]
                else:
                    if od.eng == "pe" and o.eng == "pe" and o.key is None:
                        continue
                    sem = od.eng
                    val = od.count
                if val > need.get(sem, 0):
                    need[sem] = val
            ws = []
            for sem, val in need.items():
                if val > waited.get((o.eng, sem), 0):
                    waited[(o.eng, sem)] = val
                    ws.append((sem, val))
            o.waits = ws

    def sem_names(self):
        s = set()
        for o in self.ops:
            if o.key is not None:
                s.add(("k", o.key))
            elif o.signal:
                s.add(o.eng)
        return sorted(s, key=str)

    def emit_engine(self, eng, e, sems):
        for o in self.ops:
            if o.eng != eng:
                continue
            for sem, val in o.waits:
                e.wait_ge(sems[sem], val)
            if o.fn is None:
                continue
            ins = o.fn(e)
            if o.key is not None:
                ins.then_inc(sems[("k", o.key)], 16)
            elif o.signal:
                ins.then_inc(sems[o.eng], 1)


class Stream:
    def __init__(self, name, nslots, specs=None):
        self.name, self.nslots = name, nslots
        self.record = specs is None
        self.specs = [] if specs is None else specs
        self.cur = 0
        self.issued = 0

    def get(self, P, spec, issue_fn, ok=None, live=1):
        i = self.cur
        self.cur += 1
        if self.record:
            self.specs.append(spec)
            return i % self.nslots
        assert self.specs[i] == spec, (self.name, i, self.specs[i], spec)
        lim = min(len(self.specs), i + self.nslots - live + 1)
        while self.issued < lim:
            if self.issued > i and ok is not None and not ok(self.specs[self.issued], spec):
                break
            issue_fn(P, self.specs[self.issued], self.issued % self.nslots)
            self.issued += 1
        assert self.issued > i
        return i % self.nslots


class _NullProg:
    def op(self, *a, **k):
        return 0


class StopEmit(Exception):
    pass


def _bucket(n):
    n = np.maximum(n, 0)
    me = NB_BUCKETS // 2
    large = me + (np.log(np.maximum(n, 1).astype(np.float32) / me) / math.log(128 / me)
                  * (NB_BUCKETS - me)).astype(np.int32)
    large = np.minimum(large, NB_BUCKETS - 1)
    return np.where(n < me, n, large)


def _onehots():
    j = np.arange(384)
    n = j - 127
    b = _bucket(n)
    oh_da = np.zeros((33, 384), np.float32)
    oh_sw = np.zeros((33, 384), np.float32)
    for jj in range(384):
        if n[jj] >= 0:
            oh_da[b[jj], jj] = 1.0
        else:
            oh_da[32, jj] = MASKV
        if 0 <= n[jj] < 128:
            oh_sw[b[jj], jj] = 1.0
        else:
            oh_sw[32, jj] = MASKV
    return oh_da, oh_sw


class Cfg:
    def __init__(self, S=4096, layers=(0, 1, 2, 3), final_norm=True, nl_total=DEPTH, debug=()):
        self.S = S
        self.layers = tuple(layers)
        self.final_norm = final_norm
        self.NL = nl_total
        self.debug = tuple(debug)
        self.stop = 99
        self.lam_layer = 0
        self.NG = S // GT
        self.NBLK = S // 128


def build_nc(cfg):
    nc = bass.Bass("TRN2", target_bir_lowering=False)
    S, NL, NG, NBLK = cfg.S, cfg.NL, cfg.NG, cfg.NBLK
    LAY = cfg.layers
    nlay = len(LAY)

    def din(name, shape, dt=F32):
        return nc.dram_tensor(name, list(shape), dt, kind="ExternalInput")

    Dm = {}
    Dm["x"] = din("x", [S, D])
    Dm["cT"] = din("cT", [128, 8])
    Dm["rel_bias"] = din("rel_bias", [32, 24])
    Dm["oh_da"] = din("oh_da", [33, 384])
    Dm["oh_sw"] = din("oh_sw", [33, 384])
    Dm["Jm"] = din("Jm", [128, 128])
    Dm["ident"] = din("ident", [128, 128])
    Dm["ada_w"] = din("ada_w", [NL, D, 6 * D])
    Dm["ada_b"] = din("ada_b", [NL, 6 * D])
    Dm["adabT"] = din("adabT", [NL, 128, 48])
    Dm["normg"] = din("normg", [NL, 128, 16])
    Dm["lamv"] = din("lamv", [NL, 128, 256])
    Dm["sublnT"] = din("sublnT", [NL, 128, 1])
    Dm["sinksT"] = din("sinksT", [NL, 128, 8])
    Dm["convp"] = din("convp", [NL, 128, NFF * 4])
    Dm["fgb"] = din("fgb", [128, D])
    Dm["w_in"] = din("w_in", [NL, D, IN_COLS])
    Dm["w_pa"] = din("w_pa", [NL, D, D])
    Dm["w_pb"] = din("w_pb", [NL, D, D])
    Dm["w_o"] = din("w_o", [NL, D, D])
    Dm["w_ffn_in"] = din("w_ffn_in", [NL, D, 2 * D_FF])
    Dm["w_ffn_out"] = din("w_ffn_out", [NL, D_FF, D])
    out_t = nc.dram_tensor("out", [S, D], F32, kind="ExternalOutput")
    WB = {
        "w_in": (nc.dram_tensor("wb_in", [nlay, D, IN_COLS], BF16, kind="Internal"), D, IN_COLS),
        "w_pa": (nc.dram_tensor("wb_pa", [nlay, D, D], BF16, kind="Internal"), D, D),
        "w_pb": (nc.dram_tensor("wb_pb", [nlay, D, D], BF16, kind="Internal"), D, D),
        "w_o": (nc.dram_tensor("wb_o", [nlay, D, D], BF16, kind="Internal"), D, D),
        "w_ffn_in": (nc.dram_tensor("wb_fi", [nlay, D, 2 * D_FF], BF16, kind="Internal"), D, 2 * D_FF),
        "w_ffn_out": (nc.dram_tensor("wb_fo", [nlay, D_FF, D], BF16, kind="Internal"), D_FF, D),
        "w_ksd": (nc.dram_tensor("wb_ksd", [nlay, D, 256], BF16, kind="Internal"), D, 256),
    }
    Kc = nc.dram_tensor("Kc", [8, 128, S], BF16, kind="Internal")
    Vc = nc.dram_tensor("Vc", [8, 128, NBLK, 128], BF16, kind="Internal")
    Fd = nc.dram_tensor("Fd", [24, 384], F32, kind="Internal")
    dbg_out = {}
    if "att" in cfg.debug:
        for nm_ in ("dbg_yaT", "dbg_ybT", "dbg_hT", "dbg_mg"):
            dbg_out[nm_] = nc.dram_tensor(nm_, [128, 8, GT], BF16, kind="ExternalOutput")
        dbg_out["dbg_u"] = nc.dram_tensor("dbg_u", [128, 22, GT], BF16, kind="ExternalOutput")
        dbg_out["dbg_x1"] = nc.dram_tensor("dbg_x1", [128, 4, D], F32, kind="ExternalOutput")
        dbg_out["dbg_x2"] = nc.dram_tensor("dbg_x2", [128, 4, D], F32, kind="ExternalOutput")
        dbg_out["dbg_gbc"] = nc.dram_tensor("dbg_gbc", [128, 2, D], F32, kind="ExternalOutput")

    def AP(t, off, ap):
        return bass.AP(tensor=t, offset=off, ap=[list(a) for a in ap])

    from contextlib import ExitStack
    es = ExitStack()
    with es:
        def sb(name, shape, dt):
            return es.enter_context(nc.sbuf_tensor("sb_" + name, list(shape), dt))

        Tt = sb("Tt", [128, 24, 256], F32)
        xg = sb("xg", [128, 4, D], F32)
        hT = sb("hT", [128, 8, GT], BF16)
        big = sb("big", [128, 22, GT], BF16)
        qT = big[:, 0:8, :]
        qsT = big[:, 8:16, :]
        kstage = big[:, 16:22, :]
        kst2 = sb("kst2", [128, 2, GT], BF16)
        uT = big
        vm = sb("vm", [128, 8, GT], BF16)
        vstage = vm
        mergedT = vm
        kbuf = sb("kbuf", [128, 4, 1024], BF16)
        vbuf = sb("vbuf", [128, 4, 8, 128], BF16)
        ksw = sb("ksw", [128, 2, 2, GT], BF16)
        vsw = sb("vsw", [128, 2, 4, 128], BF16)
        yaT = sb("yaT", [128, 8, GT], BF16)
        ybT = sb("ybT", [128, 8, GT], BF16)
        NW = 3
        wbuf = sb("wbuf", [128, NW, 8, GT], BF16)
        NPT = 4
        pT = sb("pT", [128, NPT, GT], BF16)
        NTMP = 6
        tmp = sb("tmp", [128, NTMP, GT], F32)
        abuf = sb("abuf", [128, 2, GT + 4], F32)
        yf = sb("yf", [128, 2, D], F32)
        gbc = sb("gbc", [128, 2, D], F32)
        identb = sb("identb", [128, 128], BF16)
        identf = sb("identf", [128, 128], F32)
        Jt = sb("Jt", [128, 128], F32)
        ones_bf = sb("ones_bf", [128, 128], BF16)
        ones_f = sb("ones_f", [128, 128], F32)
        small = sb("small", [128, 256], F32)
        carry = sb("carry", [128, NFF, 2], F32)
        convp = sb("convp", [128, NFF * 4], F32)
        lamt = sb("lamt", [128, 256], F32)
        relb = sb("relb", [33, 24], F32)
        oht = sb("oht", [33, 2, 384], F32)
        Fsb = sb("Fsb", [24, 384], F32)
        o0buf = sb("o0buf", [128, 2, GT], F32)
        adabrow = xg[0:1, 0:2, :]

        SM = {}
        _c = [0]

        def smalloc(name, n):
            SM[name] = (_c[0], n)
            _c[0] += n
            assert _c[0] <= 256

        def sm(name, i=0, n=1):
            o, _n = SM[name]
            return small[:, o + i:o + i + n]

        for nm, n in (("cact", 8), ("cneg", 8), ("modT", 32), ("adab", 48), ("normg", 16), ("A1", 8), ("A2", 8),
                      ("ss", 4), ("lnv", 4), ("rstd", 4), ("eps", 1), ("zero", 1), ("lam4", 4),
                      ("nlam", 1), ("sg", 1), ("subg", 1), ("esink", 8), ("sinks", 8), ("ss2", 4), ("lnv2", 4),
                      ("rstd2", 4), ("linit", 1), ("AB1", 16), ("AB2", 16)):
            smalloc(nm, n)

        PSB = [es.enter_context(nc.psum_tensor(f"ps{i}", [128, 512], F32)) for i in range(8)]

        def emit_all(P, WS, KS):
            def stop_at(level):
                if cfg.stop <= level:
                    raise StopEmit()

            def act(fn, reads, writes):
                P.op("act", fn, reads, writes)

            def dve(fn, reads, writes):
                P.op("dve", fn, reads, writes)

            def pe(fn, reads, writes):
                P.op("pe", fn, reads, writes)

            def dma(q, out, in_, reads, writes, key, **kw):
                P.op(q, lambda e: e.dma_start(out=out, in_=in_, **kw), reads, writes, key=key)

            def mm(out, lhsT, rhs, start, stop, reads, writes, skip=False):
                pe(lambda e: e.matmul(out, lhsT=lhsT, rhs=rhs, start=start, stop=stop,
                                      skip_group_check=skip), reads, writes)

            def A_act(out, in_, func, reads, writes, scale=None, bias=None, accum=None):
                kw = {}
                if scale is not None:
                    kw["scale"] = scale
                if bias is not None:
                    kw["bias"] = bias
                if accum is not None:
                    kw["accum_out"] = accum
                act(lambda e: e.activation(out=out, in_=in_, func=func, **kw), reads, writes)

            def V_ts(out, in0, s1, s2, op0, op1, reads, writes):
                if op1 is None:
                    dve(lambda e: e.tensor_scalar(out=out, in0=in0, scalar1=s1, scalar2=None, op0=op0),
                        reads, writes)
                else:
                    dve(lambda e: e.tensor_scalar(out=out, in0=in0, scalar1=s1, scalar2=s2, op0=op0, op1=op1),
                        reads, writes)

            def V_tt(out, in0, in1, op, reads, writes):
                dve(lambda e: e.tensor_tensor(out=out, in0=in0, in1=in1, op=op), reads, writes)

            def V_stt(out, in0, scalar, in1, op0, op1, reads, writes):
                dve(lambda e: e.scalar_tensor_tensor(out=out, in0=in0, scalar=scalar, in1=in1, op0=op0, op1=op1),
                    reads, writes)

            def V_copy(out, in_, reads, writes):
                dve(lambda e: e.tensor_copy(out=out, in_=in_), reads, writes)

            def V_recip(out, in_, reads, writes):
                dve(lambda e: e.reciprocal(out=out, in_=in_), reads, writes)

            _tmpi = [0]

            def newtmp():
                i = _tmpi[0] % NTMP
                _tmpi[0] += 1
                return i

            _pti = [0]

            def newpt():
                i = _pti[0] % NPT
                _pti[0] += 1
                return i

            def w_issue(P_, spec, slot):
                kind = spec[0]
                if kind == "bf":
                    _, name, li, r0, nrc, c0, ncols = spec
                    t, R, C = WB[name]
                    src = AP(t, li * R * C + r0 * C + c0, [[C, 128], [128 * C, nrc], [1, ncols]])
                    dst = wbuf[:, slot, 0:nrc, 0:ncols]
                    rd = [("wb", name, li)]
                else:
                    _, l, c0 = spec
                    src = AP(Dm["ada_w"], l * D * 6 * D + c0, [[6 * D, 128], [128 * 6 * D, 8], [1, 256]])
                    dst = wbuf[:, slot, :, :].bitcast(F32)
                    rd = []
                P_.op("sp", lambda e: e.dma_start(out=dst, in_=src), rd, [("w", slot)], key=f"w{slot}")

            def wget(spec, live=1):
                return WS.get(P, spec, w_issue, live=live)

            def kv_issue(P_, spec, slot):
                li, g, h, m, c = spec
                nk = min(1024, (g + 1) * GT - c * 1024)
                srck = AP(Kc, (h * 128 + m * 64) * S + c * 1024, [[S, 64], [1, nk]])
                srcv = AP(Vc, h * 128 * NBLK * 128 + c * 8 * 128, [[NBLK * 128, 128], [1, nk]])
                rd = [("Kc", gg) for gg in range(2 * c, min(2 * c + 2, g + 1))]
                rdv = [("Vc", gg) for gg in range(2 * c, min(2 * c + 2, g + 1))]
                dk = kbuf[m * 64:(m + 1) * 64, slot, 0:nk]
                dv = vbuf[:, slot, :, :].rearrange("p a b -> p (a b)")[:, 0:nk]
                P_.op("sp", lambda e: e.dma_start(out=dk, in_=srck), rd, [("kb", slot)], key=f"kv{slot}")
                P_.op("sp", lambda e: e.dma_start(out=dv, in_=srcv), rdv, [("vb", slot)], key=f"kv{slot}")

            def kvget(spec):
                return KS.get(P, spec, kv_issue, ok=lambda a, b: a[0:2] == b[0:2], live=2)

            dma("sp", identf[:], Dm["ident"].ap(), [], [("identf",)], "c0")
            dma("sp", Jt[:], Dm["Jm"].ap(), [], [("Jt",)], "c0")
            dma("sp", oht[:, 0, :], Dm["oh_da"].ap(), [], [("oht",)], "c0")
            dma("sp", oht[:, 1, :], Dm["oh_sw"].ap(), [], [("oht",)], "c0")
            dma("sp", sm("cact", 0, 8), Dm["cT"].ap(), [], [("cact",)], "c0")
            dve(lambda e: e.memset(relb[:], 1.0), [], [("relb",)])
            dma("sp", relb[0:32, :], Dm["rel_bias"].ap(), [("relb",)], [("relb",)], "c0")
            V_copy(identb[:], identf[:], [("identf",)], [("identb",)])
            dve(lambda e: e.memset(ones_bf[:], 1.0), [], [("ones_bf",)])
            dve(lambda e: e.memset(ones_f[:], 1.0), [], [("ones_f",)])
            dve(lambda e: e.memset(sm("eps"), EPS), [], [("eps",)])
            dve(lambda e: e.memset(sm("zero"), 0.0), [], [("zero",)])

            stop_at(0)
            def conv(name, li, l):
                t, R, C = WB[name]
                n = R * C
                rows = n // 1024
                step = 2048
                r = 0
                while r < rows:
                    nr = min(step, rows - r)
                    src = AP(Dm[name], l * n + r * 1024, [[1024, nr], [1, 1024]])
                    dst = AP(t, li * n + r * 1024, [[1024, nr], [1, 1024]])
                    dma("pool", dst, src, [], [("wb", name, li)], f"cv{li}{name}")
                    r += nr

            def conv_ksd(li, l):
                t, R, C = WB["w_ksd"]
                for kv in range(2):
                    for rpt in range(2):
                        src = AP(Dm["w_in"], l * D * IN_COLS + C_KS + kv * 64, [[IN_COLS, D], [1, 64]])
                        dst = AP(t, li * D * 256 + (2 * kv + rpt) * 64, [[256, D], [1, 64]])
                        dma("pool", dst, src, [], [("wb", "w_ksd", li)], f"cv{li}w_ksd")

            def conv_layer(li, l):
                conv("w_in", li, l)
                conv_ksd(li, l)
                for name in ("w_pa", "w_pb", "w_o", "w_ffn_in", "w_ffn_out"):
                    conv(name, li, l)

            conv_layer(0, LAY[0])

            stop_at(1)
            A_act(sm("cneg", 0, 8), sm("cact", 0, 8), AF.Exp, [("cact",)], [("cneg",)], scale=-1.0)
            V_ts(sm("cneg", 0, 8), sm("cneg", 0, 8), 1.0, None, ALU.add, None, [("cneg",)], [("cneg",)])
            V_recip(sm("cneg", 0, 8), sm("cneg", 0, 8), [("cneg",)], [("cneg",)])
            V_tt(sm("cact", 0, 8), sm("cact", 0, 8), sm("cneg", 0, 8), ALU.mult, [("cact",), ("cneg",)], [("cact",)])

            mm(PSB[0][0:8, 0:384], relb[:, 0:8], oht[:, 0, :], True, True,
               [("relb",), ("oht",)], [("ps", 0)])
            V_copy(Fsb[0:8, :], PSB[0][0:8, 0:384], [("ps", 0)], [("Fsb", 0)])
            mm(PSB[1][0:16, 0:384], relb[:, 8:24], oht[:, 1, :], True, True,
               [("relb",), ("oht",)], [("ps", 1)])
            V_copy(tmp[0:16, 0, 0:384], PSB[1][0:16, 0:384], [("ps", 1)], [("tmp", 0)])
            dma("sp", AP(Fd, 0, [[384, 8], [1, 384]]), Fsb[0:8, :], [("Fsb", 0)], [("Fd",)], "fd")
            dma("sp", AP(Fd, 8 * 384, [[384, 16], [1, 384]]), tmp[0:16, 0, 0:384], [("tmp", 0)], [("Fd",)], "fd")
            _tmpi[0] = 1
            for hp in range(12):
                ti = newtmp()
                Hs = tmp[:, ti, :]
                for k in range(2):
                    h = 2 * hp + k
                    dma("sp", Hs[:, k * 256:(k + 1) * 256], AP(Fd, h * 384, [[1, 128], [1, 256]]),
                        [("Fd",)], [("tmp", ti)], f"hk{ti}")
                b = hp % 2
                mm(PSB[b][:, :], Jt[:], Hs, True, True, [("Jt",), ("tmp", ti)], [("ps", b)])
                V_copy(Tt[:, 2 * hp:2 * hp + 2, :].rearrange("p a b -> p (a b)"), PSB[b][:, :],
                       [("ps", b)], [("Tt", 2 * hp), ("Tt", 2 * hp + 1)])

            stop_at(2)
            bankrot = [0]

            def nbank(pool):
                b = pool[bankrot[0] % len(pool)]
                bankrot[0] += 1
                return b

            for li, l in enumerate(LAY):
                lam_init = 0.8 - 0.6 * math.exp(-0.3 * (l + cfg.lam_layer))
                dve(lambda e: e.memset(carry[:], 0.0), [], [("carry", j) for j in range(NFF)])
                dma("sp", sm("adab", 0, 48), AP(Dm["adabT"], l * 128 * 48, [[48, 128], [1, 48]]), [], [("adab",)], "lp")
                dma("sp", sm("normg", 0, 16), AP(Dm["normg"], l * 128 * 16, [[16, 128], [1, 16]]), [], [("normg",)], "lp")
                dma("sp", lamt[:], AP(Dm["lamv"], l * 128 * 256, [[256, 128], [1, 256]]), [], [("lamt",)], "lp")
                dma("sp", sm("subg"), AP(Dm["sublnT"], l * 128, [[1, 128], [1, 1]]), [], [("subg",)], "lp")
                dma("sp", sm("sinks", 0, 8), AP(Dm["sinksT"], l * 128 * 8, [[8, 128], [1, 8]]), [], [("sinks",)], "lp")
                dma("sp", convp[:], AP(Dm["convp"], l * 128 * NFF * 4, [[NFF * 4, 128], [1, NFF * 4]]), [], [("convp",)], "lp")
                dma("sp", adabrow[:, 0, :], AP(Dm["ada_b"], l * 6 * D + 2 * D, [[D, 1], [1, D]]), [], [("xg", 0), ("xg", 1)], "lp")
                dma("sp", adabrow[:, 1, :], AP(Dm["ada_b"], l * 6 * D + 5 * D, [[D, 1], [1, D]]), [], [("xg", 0), ("xg", 1)], "lp")
                lt = lamt[:].rearrange("p (a b) -> p a b", a=4)
                V_tt(lamt[:, 0:64], lamt[:, 0:64], lamt[:, 64:128], ALU.mult, [("lamt",)], [("lamt",)])
                V_tt(lamt[:, 128:192], lamt[:, 128:192], lamt[:, 192:256], ALU.mult, [("lamt",)], [("lamt",)])
                dve(lambda e: e.reduce_sum(out=sm("lam4", 0, 1), in_=lamt[:, 0:64], axis=mybir.AxisListType.X),
                    [("lamt",)], [("lam4",)])
                dve(lambda e: e.reduce_sum(out=sm("lam4", 1, 1), in_=lamt[:, 128:192], axis=mybir.AxisListType.X),
                    [("lamt",)], [("lam4",)])
                A_act(sm("lam4", 2, 2), sm("lam4", 0, 2), AF.Exp, [("lam4",)], [("lam4",)])
                V_tt(sm("nlam"), sm("lam4", 3, 1), sm("lam4", 2, 1), ALU.subtract, [("lam4",)], [("nlam",)])
                V_ts(sm("nlam"), sm("nlam"), -lam_init, None, ALU.add, None, [("nlam",)], [("nlam",)])
                V_ts(sm("sg"), sm("subg"), 1.0 - lam_init, None, ALU.mult, None, [("subg",)], [("sg",)])
                A_act(sm("esink", 0, 8), sm("sinks", 0, 8), AF.Exp, [("sinks",)], [("esink",)])

                cbc = tmp[:, 4:6, :].rearrange("p a (b c) -> p (a b) c", c=128)
                for kc in range(8):
                    V_ts(cbc[:, kc, :], ones_f[:], sm("cact", kc, 1), None, ALU.mult, None,
                         [("ones_f",), ("cact",)], [("tmp", 4), ("tmp", 5)])
                MB = 7
                for sl in range(24):
                    c0 = sl * 256
                    slot = wget(("ada", l, c0))
                    wv = wbuf[:, slot, :, :].bitcast(F32)
                    sec = c0 // D
                    if sec in (2, 5):
                        which = 0 if sec == 2 else 1
                        cc0 = c0 - sec * D
                        b = nbank([4, 5])
                        for kc in range(8):
                            mm(PSB[b][:, 0:256], cbc[:, kc, :], wv[:, kc, :], kc == 0, False,
                               [("tmp", 4), ("tmp", 5), ("w", slot)], [("ps", b)])
                        mm(PSB[b][:, 0:256], ones_f[0:1, :], adabrow[0:1, which, cc0:cc0 + 256], False, True,
                           [("ones_f",), ("xg", 0), ("xg", 1)], [("ps", b)])
                        V_copy(gbc[:, which, cc0:cc0 + 256], PSB[b][:, 0:256], [("ps", b)], [("gbc", which)])
                    else:
                        mcol0 = {0: 0, 1: 8, 3: 16, 4: 24}[sec] + (c0 - sec * D) // 128
                        for ci in range(2):
                            for kc in range(8):
                                mm(PSB[MB][:, mcol0 + ci:mcol0 + ci + 1], wv[:, kc, ci * 128:(ci + 1) * 128],
                                   sm("cact", kc, 1), kc == 0, kc == 7,
                                   [("w", slot), ("cact",)], [("ps", MB)])
                adv = small[:, SM["adab"][0]:SM["adab"][0] + 48]
                for (mo, ao) in ((0, 0), (8, 8), (16, 24), (24, 32)):
                    V_tt(sm("modT", mo, 8), PSB[MB][:, mo:mo + 8], adv[:, ao:ao + 8], ALU.add,
                         [("ps", MB), ("adab",)], [("modT",)])
                V_stt(sm("A1", 0, 8), sm("modT", 8, 8), 1.0, sm("normg", 0, 8), ALU.add, ALU.mult,
                      [("modT",), ("normg",)], [("A1",)])
                V_stt(sm("A2", 0, 8), sm("modT", 24, 8), 1.0, sm("normg", 8, 8), ALU.add, ALU.mult,
                      [("modT",), ("normg",)], [("A2",)])
                for nm_, an_, bo_ in (("AB1", "A1", 0), ("AB2", "A2", 16)):
                    abv = sm(nm_, 0, 16).rearrange("p (k t) -> p k t", t=2)
                    V_copy(abv[:, :, 0], sm(an_, 0, 8), [(an_,)], [(nm_,)])
                    V_copy(abv[:, :, 1], sm("modT", bo_, 8), [("modT",), (nm_,)], [(nm_,)])

                stop_at(3)
                def norm_stage(Aname, Boff):
                    for t in range(4):
                        xt = xg[:, t, :]
                        yb = yf[:, t % 2, :]
                        yr = ("yf", t % 2)
                        stop_at(3.1)
                        A_act(yb, xt, AF.Square, [("xg", t)], [yr, ("ss", t)], accum=sm("ss", t, 1))
                        stop_at(3.2)
                        A_act(sm("lnv", t, 1), sm("ss", t, 1), AF.Ln, [("ss", t), ("eps",)], [("lnv", t)],
                              scale=1.0 / D, bias=sm("eps"))
                        A_act(sm("rstd", t, 1), sm("lnv", t, 1), AF.Exp, [("lnv", t)], [("rstd", t)], scale=-0.5)
                        stop_at(3.3)
                        V_ts(yb, xt, sm("rstd", t, 1), None, ALU.mult, None, [("xg", t), ("rstd", t)], [yr])
                        stop_at(3.4)
                        bb = (4, 5) if t % 2 == 0 else (6, 7)
                        for kc in range(8):
                            b = bb[kc // 4]
                            pe(lambda e, kc=kc, b=b, yb=yb: e.transpose(
                                out=PSB[b][:, (kc % 4) * 128:(kc % 4 + 1) * 128], in_=yb[:, kc * 128:(kc + 1) * 128],
                                identity=identf[:]),
                               [yr, ("identf",)], [("ps", b)])
                        stop_at(3.5)
                        for kc in range(8):
                            b = bb[kc // 4]
                            o_ = hT[:, kc, t * 128:(t + 1) * 128]
                            i_ = PSB[b][:, (kc % 4) * 128:(kc % 4 + 1) * 128]
                            if True:
                                A_act(o_, i_, AF.Identity, [("ps", b), (Aname,)], [("hT", kc, t)],
                                      scale=sm(Aname, 2 * kc, 1), bias=sm(Aname, 2 * kc + 1, 1))
                            else:
                                V_ts(o_, i_, sm(Aname, 2 * kc, 1), sm(Aname, 2 * kc + 1, 1), ALU.mult, ALU.add,
                                     [("ps", b), (Aname,)], [("hT", kc, t)])
                            stop_at(3.51 + kc * 0.01)
                        stop_at(3.6 + t * 0.01)

                def fm_proj(slot, ci, bank, rhs3, rres, nk=8):
                    for kc in range(nk):
                        mm(PSB[bank][:, :], wbuf[:, slot, kc, ci * 128:(ci + 1) * 128], rhs3[:, kc, :],
                           kc == 0, kc == nk - 1, [("w", slot)] + rres(kc), [("ps", bank)])

                hres = lambda kc: [("hT", kc, t_) for t_ in range(4)]
                evac_i = [0]

                def evac(out, in_, reads, writes):
                    evac_i[0] += 1
                    if evac_i[0] % 2:
                        A_act(out, in_, AF.Copy, reads, writes)
                    else:
                        V_copy(out, in_, reads, writes)

                def kst(h):
                    return (kstage[:, h, :], ("big", 16 + h)) if h < 6 else (kst2[:, h - 6, :], ("kst2", h - 6))

                for g in range(NG):
                    xsrc = Dm["x"] if li == 0 else out_t
                    xread = [] if li == 0 else [("xd", g * 4 + t) for t in range(4)]
                    for t in range(4):
                        dma("sp", xg[:, t, :], AP(xsrc, (g * 4 + t) * 128 * D, [[D, 128], [1, D]]),
                            [("xd", g * 4 + t)] if li else [], [("xg", t)], f"xg{t}")
                    norm_stage("AB1", 0)
                    stop_at(4)
                    gs = g % 2
                    DP = [0, 1, 2, 3, 4, 5]
                    for s2 in range(2):
                        slot = wget(("bf", "w_in", li, 0, 8, C_QA + s2 * 512, 512))
                        for ci in range(4):
                            b = nbank(DP)
                            fm_proj(slot, ci, b, hT, hres)
                            oc = s2 * 4 + ci
                            evac(qT[:, oc, :], PSB[b][:, :], [("ps", b)], [("big", oc)])
                    for s2 in range(2):
                        slot = wget(("bf", "w_in", li, 0, 8, C_KA + s2 * 512, 512))
                        for ci in range(4):
                            b = nbank(DP)
                            fm_proj(slot, ci, b, hT, hres)
                            ko, kr = kst(s2 * 4 + ci)
                            evac(ko, PSB[b][:, :], [("ps", b)], [kr])
                    dma("pool", AP(Kc, g * GT, [[S, 128], [128 * S, 6], [1, GT]]), kstage,
                        [("big", 16 + h) for h in range(6)], [("Kc", g)], "kst")
                    dma("pool", AP(Kc, 6 * 128 * S + g * GT, [[S, 128], [128 * S, 2], [1, GT]]), kst2[:],
                        [("kst2", 0), ("kst2", 1)], [("Kc", g)], "kst")
                    for s2 in range(2):
                        slot = wget(("bf", "w_in", li, 0, 8, C_VA + s2 * 512, 512))
                        for t in range(4):
                            b = nbank(DP)
                            for kc in range(8):
                                mm(PSB[b][:, :], hT[:, kc, t * 128:(t + 1) * 128], wbuf[:, slot, kc, :],
                                   kc == 0, kc == 7, [("w", slot), ("hT", kc, t)], [("ps", b)])
                            ov = vstage[:, s2 * 4:s2 * 4 + 4, t * 128:(t + 1) * 128]
                            iv = PSB[b][:, :].rearrange("p (a b) -> p a b", a=4)
                            evac(ov, iv, [("ps", b)], [("vm", s2 * 4 + k) for k in range(4)])
                    dma("pool", AP(Vc, g * 4 * 128, [[NBLK * 128, 128], [128 * NBLK * 128, 8], [1, GT]]), vstage[:],
                        [("vm", k) for k in range(8)], [("Vc", g)], "vst")
                    if g == 0 and li + 1 < nlay:
                        conv_layer(li + 1, LAY[li + 1])
                    for s2 in range(2):
                        slot = wget(("bf", "w_in", li, 0, 8, C_QS + s2 * 512, 512))
                        for ci in range(4):
                            b = nbank(DP)
                            fm_proj(slot, ci, b, hT, hres)
                            oc = s2 * 4 + ci
                            evac(qsT[:, oc, :], PSB[b][:, :], [("ps", b)], [("big", 8 + oc)])
                    slot = wget(("bf", "w_ksd", li, 0, 8, 0, 256))
                    for kv in range(2):
                        b = nbank(DP)
                        fm_proj(slot, kv, b, hT, hres)
                        evac(ksw[:, gs, kv, :], PSB[b][:, :], [("ps", b)], [("ksw", gs, kv)])
                    slot = wget(("bf", "w_in", li, 0, 8, C_VS, 128))
                    for t in range(4):
                        b = nbank(DP)
                        for kc in range(8):
                            mm(PSB[b][:, 0:128], hT[:, kc, t * 128:(t + 1) * 128], wbuf[:, slot, kc, 0:128],
                               kc == 0, kc == 7, [("w", slot), ("hT", kc, t)], [("ps", b)])
                        evac(vsw[:, gs, t, :], PSB[b][:, 0:128], [("ps", b)], [("vsw", gs, t)])

                    stop_at(5)
                    pend = []
                    SKEW = 2

                    def push(fn):
                        pend.append(fn)
                        while len(pend) > SKEW:
                            pend.pop(0)()

                    def flush():
                        while pend:
                            pend.pop(0)()

                    accsel = [0]
                    SB_ = [0, 1, 2]
                    srot = [0]

                    for ch in range(8):
                        ab = accsel[0] % 2
                        accsel[0] += 1
                        OB, UB = 3 + 2 * ab, 4 + 2 * ab
                        for half in range(2):
                            j = 2 * ch + half
                            kv = j // 8
                            p0 = half * 64
                            first = [True]
                            for jj in range(4 * g - 1, 4 * g + 4):
                                if jj < 0:
                                    continue
                                qb0, qb1 = max(jj, 4 * g), min(jj + 1, 4 * g + 3)
                                c0, c1 = (qb0 - 4 * g) * 128, (qb1 - 4 * g + 1) * 128
                                N = c1 - c0
                                if jj == 4 * g - 1:
                                    ks_, kc0, vs_, vt = 1 - gs, 384, 1 - gs, 3
                                else:
                                    ks_, kc0, vs_, vt = gs, (jj - 4 * g) * 128, gs, jj - 4 * g
                                tc0 = 0 if qb0 == jj else 128
                                sbk = SB_[srot[0] % 3]
                                srot[0] += 1
                                mm(PSB[sbk][:, 0:N], ksw[p0:p0 + 64, ks_, kv, kc0:kc0 + 128],
                                   qsT[p0:p0 + 64, ch, c0:c1], True, True,
                                   [("ksw", ks_, kv), ("big", 8 + ch)], [("ps", sbk)])
                                ti = newtmp()
                                V_stt(tmp[:, ti, 0:N], PSB[sbk][:, 0:N], SCALE, Tt[:, 8 + j, tc0:tc0 + N],
                                      ALU.mult, ALU.add, [("ps", sbk), ("Tt", 8 + j)], [("tmp", ti)])
                                pi = newpt()
                                A_act(pT[:, pi, 0:N], tmp[:, ti, 0:N], AF.Exp, [("tmp", ti)], [("pT", pi)])

                                def pv(pi=pi, N=N, c0=c0, c1=c1, vs_=vs_, vt=vt, kv=kv, p0=p0, st=first[0],
                                       OB=OB, UB=UB):
                                    mm(PSB[OB][p0:p0 + 64, c0:c1], vsw[:, vs_, vt, kv * 64:(kv + 1) * 64],
                                       pT[:, pi, 0:N], st, False, [("vsw", vs_, vt), ("pT", pi)], [("ps", OB)], skip=True)
                                    mm(PSB[UB][p0:p0 + 64, c0:c1], ones_bf[:, 0:64],
                                       pT[:, pi, 0:N], st, False, [("ones_bf",), ("pT", pi)], [("ps", UB)], skip=True)
                                push(pv)
                                first[0] = False

                        def fin_sw(ch=ch, OB=OB, UB=UB):
                            ti = newtmp()
                            V_ts(tmp[:, ti, :], PSB[UB][:, :], sm("esink", ch, 1), None, ALU.add, None,
                                 [("ps", UB), ("esink",)], [("tmp", ti)])
                            V_recip(tmp[:, ti, :], tmp[:, ti, :], [("tmp", ti)], [("tmp", ti)])
                            V_tt(ybT[:, ch, :], PSB[OB][:, :], tmp[:, ti, :], ALU.mult,
                                 [("ps", OB), ("tmp", ti)], [("ybT", ch)])
                        push(fin_sw)

                    flush()
                    stop_at(6)
                    nkb = 4 * (g + 1)
                    nch = (nkb + 7) // 8
                    for h in range(8):
                        cb = Tt[:, h, 255:256]
                        hold = [None]
                        for m in range(2):
                            ab = accsel[0] % 2
                            accsel[0] += 1
                            OB, UB = 3 + 2 * ab, 4 + 2 * ab
                            p0 = m * 64
                            for kb in range(nkb):
                                kk = kb % 8
                                if kk == 0:
                                    slot = kvget((li, g, h, m, kb // 8))
                                qb0 = max(kb, 4 * g)
                                c0 = (qb0 - 4 * g) * 128
                                N = GT - c0
                                sbk = SB_[srot[0] % 3]
                                srot[0] += 1
                                mm(PSB[sbk][:, 0:N], kbuf[p0:p0 + 64, slot, kk * 128:(kk + 1) * 128],
                                   qT[p0:p0 + 64, h, c0:GT], True, True,
                                   [("kb", slot), ("big", h)], [("ps", sbk)])
                                pi = newpt()
                                if kb < 4 * g - 1:
                                    A_act(pT[:, pi, 0:N], PSB[sbk][:, 0:N], AF.Exp, [("ps", sbk), ("Tt", h)],
                                          [("pT", pi)], scale=SCALE, bias=cb)
                                else:
                                    if kb == 4 * g - 1:
                                        nb, tc0 = 128, 128
                                    else:
                                        nb, tc0 = min(256, N), 0
                                    ti = newtmp()
                                    V_stt(tmp[:, ti, 0:nb], PSB[sbk][:, 0:nb], SCALE, Tt[:, h, tc0:tc0 + nb],
                                          ALU.mult, ALU.add, [("ps", sbk), ("Tt", h)], [("tmp", ti)])
                                    A_act(pT[:, pi, 0:nb], tmp[:, ti, 0:nb], AF.Exp, [("tmp", ti)], [("pT", pi)])
                                    if N > nb:
                                        A_act(pT[:, pi, nb:N], PSB[sbk][:, nb:N], AF.Exp, [("ps", sbk), ("Tt", h)],
                                              [("pT", pi)], scale=SCALE, bias=cb)

                                def pv(pi=pi, N=N, c0=c0, slot=slot, kk=kk, st=(kb == 0), OB=OB, UB=UB):
                                    mm(PSB[OB][:, c0:GT], vbuf[:, slot, kk, :], pT[:, pi, 0:N], st, False,
                                       [("vb", slot), ("pT", pi)], [("ps", OB)], skip=True)
                                    mm(PSB[UB][:, c0:GT], ones_bf[:], pT[:, pi, 0:N], st, False,
                                       [("ones_bf",), ("pT", pi)], [("ps", UB)], skip=True)
                                push(pv)

                            def fin_da(m=m, h=h, OB=OB, UB=UB, hold=hold):
                                tr = newtmp()
                                V_recip(tmp[:, tr, :], PSB[UB][:, :], [("ps", UB)], [("tmp", tr)])
                                o0 = o0buf[:, h % 2, :]
                                o0r = ("o0", h % 2)
                                if m == 0:
                                    V_tt(o0, PSB[OB][:, :], tmp[:, tr, :], ALU.mult,
                                         [("ps", OB), ("tmp", tr)], [o0r])
                                else:
                                    V_tt(tmp[:, tr, :], PSB[OB][:, :], tmp[:, tr, :], ALU.mult,
                                         [("ps", OB), ("tmp", tr)], [("tmp", tr)])
                                    V_stt(o0, tmp[:, tr, :], sm("nlam"), o0, ALU.mult, ALU.add,
                                          [("tmp", tr), o0r, ("nlam",)], [o0r])
                                    A_act(tmp[:, tr, :], o0, AF.Square, [o0r], [("tmp", tr)])
                                    mm(PSB[7][:, :], ones_f[:], tmp[:, tr, :], True, True,
                                       [("ones_f",), ("tmp", tr)], [("ps", 7)])
                                    A_act(tmp[:, tr, :], PSB[7][:, :], AF.Ln, [("ps", 7), ("eps",)], [("tmp", tr)],
                                          scale=1.0 / 128, bias=sm("eps"))
                                    A_act(tmp[:, tr, :], tmp[:, tr, :], AF.Exp, [("tmp", tr)], [("tmp", tr)], scale=-0.5)
                                    V_stt(yaT[:, h, :], o0, sm("sg"), tmp[:, tr, :], ALU.mult, ALU.mult,
                                          [o0r, ("tmp", tr), ("sg",)], [("yaT", h)])
                            push(fin_da)
                    flush()
                    if "att" in cfg.debug and g == cfg.NG - 1 and li == 0:
                        for nm_, t_, rs_ in (("dbg_yaT", yaT, [("yaT", k) for k in range(8)]),
                                             ("dbg_ybT", ybT, [("ybT", k) for k in range(8)]),
                                             ("dbg_hT", hT, [("hT", k, t_) for k in range(8) for t_ in range(4)])):
                            dt_ = dbg_out[nm_]
                            dma("pool", dt_.ap(), t_[:], rs_, [("dbg", nm_)], "dbg")

                    stop_at(7)
                    for s2 in range(2):
                        slot = wget(("bf", "w_pa", li, 0, 8, s2 * 512, 512))
                        for ci in range(4):
                            fm_proj(slot, ci, ci, yaT, lambda kc: [("yaT", kc)])
                        slot = wget(("bf", "w_in", li, 0, 8, C_GA + s2 * 512, 512))
                        for ci in range(4):
                            fm_proj(slot, ci, 4 + ci, hT, hres)
                        for ci in range(4):
                            A_act(tmp[:, ci, :], PSB[4 + ci][:, :], AF.Sigmoid, [("ps", 4 + ci)], [("tmp", ci)])
                            V_tt(tmp[:, ci, :], PSB[ci][:, :], tmp[:, ci, :], ALU.mult,
                                 [("ps", ci), ("tmp", ci)], [("tmp", ci)])
                        slot = wget(("bf", "w_pb", li, 0, 8, s2 * 512, 512))
                        for ci in range(4):
                            fm_proj(slot, ci, ci, ybT, lambda kc: [("ybT", kc)])
                        slot = wget(("bf", "w_in", li, 0, 8, C_GB + s2 * 512, 512))
                        for ci in range(4):
                            fm_proj(slot, ci, 4 + ci, hT, hres)
                        for ci in range(4):
                            tb = 4 + (ci % 2)
                            A_act(tmp[:, tb, :], PSB[4 + ci][:, :], AF.Sigmoid, [("ps", 4 + ci)], [("tmp", tb)])
                            V_tt(tmp[:, tb, :], PSB[ci][:, :], tmp[:, tb, :], ALU.mult,
                                 [("ps", ci), ("tmp", tb)], [("tmp", tb)])
                            V_tt(mergedT[:, s2 * 4 + ci, :], tmp[:, ci, :], tmp[:, tb, :], ALU.add,
                                 [("tmp", ci), ("tmp", tb)], [("vm", s2 * 4 + ci)])
                    dbgon = "att" in cfg.debug and g == cfg.NG - 1 and li == 0
                    if dbgon:
                        dma("pool", dbg_out["dbg_mg"].ap(), mergedT[:], [("vm", k) for k in range(8)], [("dbg", "mg")], "dbg")
                        dma("pool", dbg_out["dbg_gbc"].ap(), gbc[:], [("gbc", 0), ("gbc", 1)], [("dbg", "gbc")], "dbg")
                    AP8 = [0, 1, 2, 3, 4, 5, 6, 7]
                    for chh in range(2):
                        slot = wget(("bf", "w_o", li, 0, 8, chh * 512, 512))
                        for t in range(4):
                            b = nbank(AP8)
                            for cc in range(8):
                                mm(PSB[b][:, :], mergedT[:, cc, t * 128:(t + 1) * 128], wbuf[:, slot, cc, :],
                                   cc == 0, cc == 7, [("vm", cc), ("w", slot)], [("ps", b)])
                            ti = newtmp()
                            V_tt(tmp[:, ti, :], PSB[b][:, :], gbc[:, 0, chh * 512:(chh + 1) * 512], ALU.mult,
                                 [("ps", b), ("gbc", 0)], [("tmp", ti)])
                            xs = xg[:, t, chh * 512:(chh + 1) * 512]
                            V_tt(xs, xs, tmp[:, ti, :], ALU.add, [("xg", t), ("tmp", ti)], [("xg", t)])
                    stop_at(8)
                    if dbgon:
                        dma("pool", dbg_out["dbg_x1"].ap(), xg[:], [("xg", k) for k in range(4)], [("dbg", "x1")], "dbg")
                    norm_stage("AB2", 16)
                    slotA = slotB = None
                    for j in range(NFF):
                        s2, ci = j // 4, j % 4
                        ncol = 512 if s2 < 5 else 256
                        if ci == 0:
                            slotA = wget(("bf", "w_ffn_in", li, 0, 8, s2 * 512, ncol))
                            slotB = wget(("bf", "w_ffn_in", li, 0, 8, D_FF + s2 * 512, ncol), live=2)
                        bA, bB = [(0, 1), (2, 3), (4, 5)][j % 3]
                        fm_proj(slotA, ci, bA, hT, hres)
                        fm_proj(slotB, ci, bB, hT, hres)
                        ab_ = abuf[:, j % 2, :]
                        abr = ("abuf", j % 2)
                        P.op("pool", lambda e, ab_=ab_, j=j: e.tensor_copy(out=ab_[:, 0:2], in_=carry[:, j, :]),
                             [("carry", j)], [abr])
                        A_act(ab_[:, 2:2 + GT], PSB[bA][:, :], AF.Copy, [("ps", bA)], [abr])
                        P.op("pool", lambda e, ab_=ab_, j=j: e.tensor_copy(out=carry[:, j, :], in_=ab_[:, GT:GT + 2]),
                             [abr], [("carry", j)])
                        cw = lambda k, j=j: convp[:, j * 4 + k:j * 4 + k + 1]
                        ti = newtmp()
                        A_act(tmp[:, ti, :], ab_[:, 2:2 + GT], AF.Identity, [abr, ("convp",)], [("tmp", ti)],
                              scale=cw(2), bias=cw(3))
                        V_stt(tmp[:, ti, :], ab_[:, 1:1 + GT], cw(1), tmp[:, ti, :], ALU.mult, ALU.add,
                              [abr, ("convp",), ("tmp", ti)], [("tmp", ti)])
                        V_stt(tmp[:, ti, :], ab_[:, 0:GT], cw(0), tmp[:, ti, :], ALU.mult, ALU.add,
                              [abr, ("convp",), ("tmp", ti)], [("tmp", ti)])
                        A_act(tmp[:, ti, :], tmp[:, ti, :], AF.Silu, [("tmp", ti)], [("tmp", ti)])
                        V_tt(uT[:, j, :], PSB[bB][:, :], tmp[:, ti, :], ALU.mult,
                             [("ps", bB), ("tmp", ti)], [("big", j)])
                    last_layer = (li == nlay - 1)
                    if dbgon:
                        dma("pool", dbg_out["dbg_u"].ap(), uT[:], [("big", k) for k in range(22)], [("dbg", "u")], "dbg")
                    for chh in range(2):
                        accb = [0, 1, 2, 3] if chh == 0 else [4, 5, 6, 7]
                        for s3 in range(3):
                            j0 = s3 * 8
                            nj = min(8, NFF - j0)
                            slot = wget(("bf", "w_ffn_out", li, j0 * 128, nj, chh * 512, 512))
                            for t in range(4):
                                for jj in range(nj):
                                    j = j0 + jj
                                    mm(PSB[accb[t]][:, :], uT[:, j, t * 128:(t + 1) * 128], wbuf[:, slot, jj, :],
                                       j == 0, j == NFF - 1, [("big", j), ("w", slot)], [("ps", accb[t])])
                        for t in range(4):
                            ti = newtmp()
                            V_tt(tmp[:, ti, :], PSB[accb[t]][:, :], gbc[:, 1, chh * 512:(chh + 1) * 512], ALU.mult,
                                 [("ps", accb[t]), ("gbc", 1)], [("tmp", ti)])
                            xs = xg[:, t, chh * 512:(chh + 1) * 512]
                            V_tt(xs, xs, tmp[:, ti, :], ALU.add, [("xg", t), ("tmp", ti)], [("xg", t)])
                    if dbgon:
                        dma("pool", dbg_out["dbg_x2"].ap(), xg[:], [("xg", k) for k in range(4)], [("dbg", "x2")], "dbg")
                    for t in range(4):
                        dst = AP(out_t, (g * 4 + t) * 128 * D, [[D, 128], [1, D]])
                        if last_layer and cfg.final_norm:
                            xt = xg[:, t, :]
                            A_act(yf[:, t % 2, :], xt, AF.Square, [("xg", t)], [("yf", t % 2), ("ss2", t)], accum=sm("ss2", t, 1))
                            A_act(sm("lnv2", t, 1), sm("ss2", t, 1), AF.Ln, [("ss2", t), ("eps",)], [("lnv2", t)],
                                  scale=1.0 / D, bias=sm("eps"))
                            A_act(sm("rstd2", t, 1), sm("lnv2", t, 1), AF.Exp, [("lnv2", t)], [("rstd2", t)], scale=-0.5)
                            V_stt(xt, xt, sm("rstd2", t, 1), fgbt[:], ALU.mult, ALU.mult,
                                  [("xg", t), ("rstd2", t), ("fgb",)], [("xg", t)])
                        dma("pool", dst, xg[:, t, :], [("xg", t)], [("xd", g * 4 + t)], f"xs{t}")
            P.op("pool", None, [("xd", i) for i in range(NBLK)] + [("dbg", k) for k in dbg_out], [])

        fgbt = sb("fgbt", [128, D], F32)

        WS1, KS1 = Stream("w", NW), Stream("kv", 4)
        try:
            emit_all(_NullProg(), WS1, KS1)
        except StopEmit:
            pass
        P = Prog()
        P.op("sp", lambda e: e.dma_start(out=fgbt[:], in_=Dm["fgb"].ap()), [], [("fgb",)], key="c0")
        WS2, KS2 = Stream("w", NW, WS1.specs), Stream("kv", 4, KS1.specs)
        try:
            emit_all(P, WS2, KS2)
        except StopEmit:
            pass
        P.final_wait("pool")
        P.finalize()
        semnames = P.sem_names()
        sems = {}
        for i, sname in enumerate(semnames):
            sems[sname] = es.enter_context(nc.semaphore(f"s{i}"))
        block = es.enter_context(nc.Block())

        @block.tensor
        def _(e):
            P.emit_engine("pe", e, sems)

        @block.scalar
        def _(e):
            P.emit_engine("act", e, sems)

        @block.vector
        def _(e):
            P.emit_engine("dve", e, sems)

        @block.gpsimd
        def _(e):
            P.emit_engine("pool", e, sems)

        @block.sync
        def _(e):
            P.emit_engine("sp", e, sems)
    return nc, sorted(dbg_out.keys())


def _host_layouts(inp, nl):
    f = np.float32
    L = nl
    oh_da, oh_sw = _onehots()
    shared = {
        "rel_bias": np.ascontiguousarray(inp["rel_bias"], f),
        "oh_da": oh_da, "oh_sw": oh_sw,
        "Jm": np.ascontiguousarray(np.eye(128, dtype=f)[::-1]),
        "ident": np.eye(128, dtype=f),
        "ada_w": np.ascontiguousarray(inp["ada_w"], f),
        "ada_b": np.ascontiguousarray(inp["ada_b"], f),
        "adabT": np.ascontiguousarray(np.asarray(inp["ada_b"], f).reshape(L, 48, 128).transpose(0, 2, 1)),
        "normg": np.ascontiguousarray(np.concatenate(
            [np.asarray(inp["norm_mix_g"], f).reshape(L, 8, 128).transpose(0, 2, 1),
             np.asarray(inp["norm_ffn_g"], f).reshape(L, 8, 128).transpose(0, 2, 1)], axis=2)),
        "lamv": np.ascontiguousarray(np.broadcast_to(np.concatenate(
            [np.asarray(inp[k], f) for k in ("lam_q1", "lam_k1", "lam_q2", "lam_k2")], axis=1)[:, None, :],
            (L, 128, 256))),
        "sublnT": np.ascontiguousarray(np.asarray(inp["subln_g"], f).reshape(L, 128, 1)),
        "sinksT": np.ascontiguousarray(np.repeat(np.asarray(inp["sinks"], f).reshape(L, 8, 2).transpose(0, 2, 1),
                                                 64, axis=1)),
        "convp": np.ascontiguousarray(np.concatenate(
            [np.asarray(inp["conv_w"], f).reshape(L, 3, NFF, 128).transpose(0, 3, 2, 1),
             np.asarray(inp["conv_b"], f).reshape(L, NFF, 128).transpose(0, 2, 1)[..., None]], axis=3
        ).reshape(L, 128, NFF * 4)),
        "fgb": np.ascontiguousarray(np.broadcast_to(np.asarray(inp["final_g"], f)[None, :], (128, D))),
    }
    for k in ("w_in", "w_pa", "w_pb", "w_o", "w_ffn_in", "w_ffn_out"):
        shared[k] = np.ascontiguousarray(inp[k], f)
    return shared


def _core_inputs(shared, x_b, c_b):
    m = dict(shared)
    m["x"] = np.ascontiguousarray(x_b, np.float32)
    m["cT"] = np.ascontiguousarray(np.asarray(c_b, np.float32).reshape(8, 128).T)
    return m


_NC_CACHE = {}


def _get_nc(S, layers, final_norm, debug=()):
    key = (S, tuple(layers), final_norm, tuple(debug))
    if key not in _NC_CACHE:
        _NC_CACHE[key] = build_nc(Cfg(S=S, layers=layers, final_norm=final_norm, debug=debug))
    return _NC_CACHE[key]


MODE = "fused"


def kernel(**inputs):
    x = np.asarray(inputs["x"], np.float32)
    c = np.asarray(inputs["c"], np.float32)
    B, S, _ = x.shape
    shared = _host_layouts(inputs, DEPTH)
    cores = list(range(B))
    if MODE == "fused":
        nc, _ = build_nc(Cfg(S=S, layers=range(DEPTH), final_norm=True))
        in_maps = [_core_inputs(shared, x[b], c[b]) for b in range(B)]
        res = run_bass_kernel_spmd(nc, in_maps, core_ids=cores)
        return np.stack([np.asarray(r["out"]) for r in res.results], axis=0).astype(np.float32)
    cur = [x[b] for b in range(B)]
    per_layer = ("ada_w", "ada_b", "norm_mix_g", "norm_ffn_g", "w_in", "lam_q1", "lam_k1", "lam_q2", "lam_k2",
                 "subln_g", "sinks", "w_pa", "w_pb", "w_o", "w_ffn_in", "conv_w", "conv_b", "w_ffn_out")
    for l in range(DEPTH):
        sub = dict(inputs)
        for k in per_layer:
            sub[k] = np.asarray(inputs[k])[l:l + 1]
        shared_l = _host_layouts(sub, 1)
        cfg = Cfg(S=S, layers=(0,), final_norm=(l == DEPTH - 1), nl_total=1)
        cfg.lam_layer = l
        nc, _ = build_nc(cfg)
        in_maps = [_core_inputs(shared_l, cur[b], c[b]) for b in range(B)]
        res = run_bass_kernel_spmd(nc, in_maps, core_ids=cores)
        cur = [np.asarray(r["out"]) for r in res.results]
    return np.stack(cur, axis=0).astype(np.float32)
```

```python
import math
import bisect
import numpy as np
import concourse.bass as bass
import concourse.mybir as mybir
from concourse.bass_utils import run_bass_kernel_spmd

F32 = mybir.dt.float32
BF16 = mybir.dt.bfloat16
AF = mybir.ActivationFunctionType
ALU = mybir.AluOpType

D = 1024
DEPTH = 4
HD = 64
NB_BUCKETS = 32
D_FF = 2816
NFF = D_FF // 128
IN_COLS = 6400
EPS = 1e-6
GT = 512
MASKV = -30000.0
SCALE = HD ** -0.5

C_QA, C_KA, C_VA, C_QS, C_KS, C_VS, C_GA, C_GB = 0, 1024, 2048, 3072, 4096, 4224, 4352, 5376


class _Op:
    __slots__ = ("eng", "fn", "deps", "key", "count", "signal", "idx", "waits")


class Prog:
    def __init__(self):
        self.ops = []
        self.lastw = {}
        self.readers = {}

    def op(self, eng, fn, reads=(), writes=(), key=None):
        idx = len(self.ops)
        deps = set()
        for r in reads:
            w = self.lastw.get(r)
            if w is not None:
                deps.add(w)
        for w_ in writes:
            w = self.lastw.get(w_)
            if w is not None:
                deps.add(w)
            rl = self.readers.get(w_)
            if rl:
                deps.update(rl)
        for r in reads:
            self.readers.setdefault(r, []).append(idx)
        for w_ in writes:
            self.lastw[w_] = idx
            self.readers[w_] = []
        o = _Op()
        o.eng, o.fn, o.deps, o.key, o.idx = eng, fn, deps, key, idx
        o.count, o.signal, o.waits = 0, False, None
        self.ops.append(o)
        return idx

    def final_wait(self, eng):
        last = {}
        for o in self.ops:
            if o.fn is None:
                continue
            last[("k", o.key) if o.key is not None else o.eng] = o.idx
        idx = len(self.ops)
        o = _Op()
        o.eng, o.fn, o.deps, o.key, o.idx = eng, None, set(last.values()), None, idx
        o.count, o.signal, o.waits = 0, False, None
        self.ops.append(o)

    def finalize(self):
        ops = self.ops
        for o in ops:
            for d in o.deps:
                od = ops[d]
                if od.key is not None:
                    continue
                if od.eng == "pe" and o.eng == "pe" and o.key is None:
                    continue
                od.signal = True
        cnt = {}
        self.keylist = {}
        for o in ops:
            if o.key is not None:
                c = cnt.get(("k", o.key), 0) + 16
                cnt[("k", o.key)] = c
                o.count = c
                self.keylist.setdefault(o.key, ([], []))
                self.keylist[o.key][0].append(o.idx)
                self.keylist[o.key][1].append(c)
            elif o.signal:
                c = cnt.get(o.eng, 0) + 1
                cnt[o.eng] = c
                o.count = c
        waited = {}
        for o in ops:
            need = {}
            for d in o.deps:
                od = ops[d]
                if od.key is not None:
                    idxs, cums = self.keylist[od.key]
                    p = bisect.bisect_left(idxs, o.idx) - 1
                    sem = ("k", od.key)
                    val = cums[p]
                else:
                    if od.eng == "pe" and o.eng == "pe" and o.key is None:
                        continue
                    sem = od.eng
                    val = od.count
                if val > need.get(sem, 0):
                    need[sem] = val
            ws = []
            for sem, val in need.items():
                if val > waited.get((o.eng, sem), 0):
                    waited[(o.eng, sem)] = val
                    ws.append((sem, val))
            o.waits = ws

    def sem_names(self):
        s = set()
        for o in self.ops:
            if o.key is not None:
                s.add(("k", o.key))
            elif o.signal:
                s.add(o.eng)
        return sorted(s, key=str)

    def emit_engine(self, eng, e, sems):
        for o in self.ops:
            if o.eng != eng:
                continue
            for sem, val in o.waits:
                e.wait_ge(sems[sem], val)
            if o.fn is None:
                continue
            ins = o.fn(e)
            if o.key is not None:
                ins.then_inc(sems[("k", o.key)], 16)
            elif o.signal:
                ins.then_inc(sems[o.eng], 1)


class Stream:
    def __init__(self, name, nslots, specs=None):
        self.name, self.nslots = name, nslots
        self.record = specs is None
        self.specs = [] if specs is None else specs
        self.cur = 0
        self.issued = 0

    def get(self, P, spec, issue_fn, ok=None, live=1):
        i = self.cur
        self.cur += 1
        if self.record:
            self.specs.append(spec)
            return i % self.nslots
        assert self.specs[i] == spec, (self.name, i, self.specs[i], spec)
        lim = min(len(self.specs), i + self.nslots - live + 1)
        while self.issued < lim:
            if self.issued > i and ok is not None and not ok(self.specs[self.issued], spec):
                break
            issue_fn(P, self.specs[self.issued], self.issued % self.nslots)
            self.issued += 1
        assert self.issued > i
        return i % self.nslots


class _NullProg:
    def op(self, *a, **k):
        return 0


class StopEmit(Exception):
    pass


def _bucket(n):
    n = np.maximum(n, 0)
    me = NB_BUCKETS // 2
    large = me + (np.log(np.maximum(n, 1).astype(np.float32) / me) / math.log(128 / me)
                  * (NB_BUCKETS - me)).astype(np.int32)
    large = np.minimum(large, NB_BUCKETS - 1)
    return np.where(n < me, n, large)


def _onehots():
    j = np.arange(384)
    n = j - 127
    b = _bucket(n)
    oh_da = np.zeros((33, 384), np.float32)
    oh_sw = np.zeros((33, 384), np.float32)
    for jj in range(384):
        if n[jj] >= 0:
            oh_da[b[jj], jj] = 1.0
        else:
            oh_da[32, jj] = MASKV
        if 0 <= n[jj] < 128:
            oh_sw[b[jj], jj] = 1.0
        else:
            oh_sw[32, jj] = MASKV
    return oh_da, oh_sw


class Cfg:
    def __init__(self, S=4096, layers=(0, 1, 2, 3), final_norm=True, nl_total=DEPTH, debug=()):
        self.S = S
        self.layers = tuple(layers)
        self.final_norm = final_norm
        self.NL = nl_total
        self.debug = tuple(debug)
        self.stop = 99
        self.lam_layer = 0
        self.NG = S // GT
        self.NBLK = S // 128


def build_nc(cfg):
    nc = bass.Bass("TRN2", target_bir_lowering=False)
    S, NL, NG, NBLK = cfg.S, cfg.NL, cfg.NG, cfg.NBLK
    LAY = cfg.layers
    nlay = len(LAY)

    def din(name, shape, dt=F32):
        return nc.dram_tensor(name, list(shape), dt, kind="ExternalInput")

    Dm = {}
    Dm["x"] = din("x", [S, D])
    Dm["cT"] = din("cT", [128, 8])
    Dm["rel_bias"] = din("rel_bias", [32, 24])
    Dm["oh_da"] = din("oh_da", [33, 384])
    Dm["oh_sw"] = din("oh_sw", [33, 384])
    Dm["Jm"] = din("Jm", [128, 128])
    Dm["ident"] = din("ident", [128, 128])
    Dm["ada_w"] = din("ada_w", [NL, D, 6 * D])
    Dm["ada_b"] = din("ada_b", [NL, 6 * D])
    Dm["adabT"] = din("adabT", [NL, 128, 48])
    Dm["normg"] = din("normg", [NL, 128, 16])
    Dm["lamv"] = din("lamv", [NL, 128, 256])
    Dm["sublnT"] = din("sublnT", [NL, 128, 1])
    Dm["sinksT"] = din("sinksT", [NL, 128, 8])
    Dm["convp"] = din("convp", [NL, 128, NFF * 4])
    Dm["fgb"] = din("fgb", [128, D])
    Dm["w_in"] = din("w_in", [NL, D, IN_COLS])
    Dm["w_pa"] = din("w_pa", [NL, D, D])
    Dm["w_pb"] = din("w_pb", [NL, D, D])
    Dm["w_o"] = din("w_o", [NL, D, D])
    Dm["w_ffn_in"] = din("w_ffn_in", [NL, D, 2 * D_FF])
    Dm["w_ffn_out"] = din("w_ffn_out", [NL, D_FF, D])
    out_t = nc.dram_tensor("out", [S, D], F32, kind="ExternalOutput")
    WB = {
        "w_in": (nc.dram_tensor("wb_in", [nlay, D, IN_COLS], BF16, kind="Internal"), D, IN_COLS),
        "w_pa": (nc.dram_tensor("wb_pa", [nlay, D, D], BF16, kind="Internal"), D, D),
        "w_pb": (nc.dram_tensor("wb_pb", [nlay, D, D], BF16, kind="Internal"), D, D),
        "w_o": (nc.dram_tensor("wb_o", [nlay, D, D], BF16, kind="Internal"), D, D),
        "w_ffn_in": (nc.dram_tensor("wb_fi", [nlay, D, 2 * D_FF], BF16, kind="Internal"), D, 2 * D_FF),
        "w_ffn_out": (nc.dram_tensor("wb_fo", [nlay, D_FF, D], BF16, kind="Internal"), D_FF, D),
        "w_ksd": (nc.dram_tensor("wb_ksd", [nlay, D, 256], BF16, kind="Internal"), D, 256),
    }
    Kc = nc.dram_tensor("Kc", [8, 128, S], BF16, kind="Internal")
    Vc = nc.dram_tensor("Vc", [8, 128, NBLK, 128], BF16, kind="Internal")
    Fd = nc.dram_tensor("Fd", [24, 384], F32, kind="Internal")
    dbg_out = {}
    if "att" in cfg.debug:
        for nm_ in ("dbg_yaT", "dbg_ybT", "dbg_hT", "dbg_mg"):
            dbg_out[nm_] = nc.dram_tensor(nm_, [128, 8, GT], BF16, kind="ExternalOutput")
        dbg_out["dbg_u"] = nc.dram_tensor("dbg_u", [128, 22, GT], BF16, kind="ExternalOutput")
        dbg_out["dbg_x1"] = nc.dram_tensor("dbg_x1", [128, 4, D], F32, kind="ExternalOutput")
        dbg_out["dbg_x2"] = nc.dram_tensor("dbg_x2", [128, 4, D], F32, kind="ExternalOutput")
        dbg_out["dbg_gbc"] = nc.dram_tensor("dbg_gbc", [128, 2, D], F32, kind="ExternalOutput")

    def AP(t, off, ap):
        return bass.AP(tensor=t, offset=off, ap=[list(a) for a in ap])

    from contextlib import ExitStack
    es = ExitStack()
    with es:
        def sb(name, shape, dt):
            return es.enter_context(nc.sbuf_tensor("sb_" + name, list(shape), dt))

        Tt = sb("Tt", [128, 24, 256], F32)
        xg = sb("xg", [128, 4, D], F32)
        hT = sb("hT", [128, 8, GT], BF16)
        big = sb("big", [128, 22, GT], BF16)
        qT = big[:, 0:8, :]
        qsT = big[:, 8:16, :]
        kstage = big[:, 16:22, :]
        kst2 = sb("kst2", [128, 2, GT], BF16)
        uT = big
        vm = sb("vm", [128, 8, GT], BF16)
        vstage = vm
        mergedT = vm
        kbuf = sb("kbuf", [128, 4, 1024], BF16)
        vbuf = sb("vbuf", [128, 4, 8, 128], BF16)
        ksw = sb("ksw", [128, 2, 2, GT], BF16)
        vsw = sb("vsw", [128, 2, 4, 128], BF16)
        yaT = sb("yaT", [128, 8, GT], BF16)
        ybT = sb("ybT", [128, 8, GT], BF16)
        NW = 4
        wbuf = sb("wbuf", [128, NW, 8, GT], BF16)
        NPT = 5
        pT = sb("pT", [128, NPT, GT], BF16)
        NTMP = 6
        tmp = sb("tmp", [128, NTMP, GT], F32)
        abuf = sb("abuf", [128, 2, GT + 4], F32)
        yf = sb("yf", [128, 2, D], F32)
        gbc = sb("gbc", [128, 2, D], F32)
        identb = sb("identb", [128, 128], BF16)
        identf = sb("identf", [128, 128], F32)
        Jt = sb("Jt", [128, 128], F32)
        ones_bf = sb("ones_bf", [128, 128], BF16)
        ones_f = sb("ones_f", [128, 128], F32)
        small = sb("small", [128, 256], F32)
        carry = sb("carry", [128, NFF, 2], F32)
        convp = sb("convp", [128, NFF * 4], F32)
        lamt = sb("lamt", [128, 256], F32)
        relb = sb("relb", [33, 24], F32)
        oht = sb("oht", [33, 2, 384], F32)
        Fsb = sb("Fsb", [24, 384], F32)
        o0buf = sb("o0buf", [128, 2, GT], F32)
        adabrow = xg[0:1, 0:2, :]

        SM = {}
        _c = [0]

        def smalloc(name, n):
            SM[name] = (_c[0], n)
            _c[0] += n
            assert _c[0] <= 256

        def sm(name, i=0, n=1):
            o, _n = SM[name]
            return small[:, o + i:o + i + n]

        for nm, n in (("cact", 8), ("cneg", 8), ("modT", 32), ("adab", 48), ("normg", 16), ("A1", 8), ("A2", 8),
                      ("ss", 4), ("lnv", 4), ("rstd", 4), ("eps", 1), ("zero", 1), ("lam4", 4),
                      ("nlam", 1), ("sg", 1), ("subg", 1), ("esink", 8), ("sinks", 8), ("ss2", 4), ("lnv2", 4),
                      ("rstd2", 4), ("linit", 1), ("AB1", 16), ("AB2", 16)):
            smalloc(nm, n)

        PSB = [es.enter_context(nc.psum_tensor(f"ps{i}", [128, 512], F32)) for i in range(8)]

        def emit_all(P, WS, KS):
            def stop_at(level):
                if cfg.stop <= level:
                    raise StopEmit()

            def act(fn, reads, writes):
                P.op("act", fn, reads, writes)

            def dve(fn, reads, writes):
                P.op("dve", fn, reads, writes)

            def pe(fn, reads, writes):
                P.op("pe", fn, reads, writes)

            def dma(q, out, in_, reads, writes, key, **kw):
                P.op(q, lambda e: e.dma_start(out=out, in_=in_, **kw), reads, writes, key=key)

            def mm(out, lhsT, rhs, start, stop, reads, writes, skip=False):
                pe(lambda e: e.matmul(out, lhsT=lhsT, rhs=rhs, start=start, stop=stop,
                                      skip_group_check=skip), reads, writes)

            def A_act(out, in_, func, reads, writes, scale=None, bias=None, accum=None):
                kw = {}
                if scale is not None:
                    kw["scale"] = scale
                if bias is not None:
                    kw["bias"] = bias
                if accum is not None:
                    kw["accum_out"] = accum
                act(lambda e: e.activation(out=out, in_=in_, func=func, **kw), reads, writes)

            def V_ts(out, in0, s1, s2, op0, op1, reads, writes):
                if op1 is None:
                    dve(lambda e: e.tensor_scalar(out=out, in0=in0, scalar1=s1, scalar2=None, op0=op0),
                        reads, writes)
                else:
                    dve(lambda e: e.tensor_scalar(out=out, in0=in0, scalar1=s1, scalar2=s2, op0=op0, op1=op1),
                        reads, writes)

            def V_tt(out, in0, in1, op, reads, writes):
                dve(lambda e: e.tensor_tensor(out=out, in0=in0, in1=in1, op=op), reads, writes)

            def V_stt(out, in0, scalar, in1, op0, op1, reads, writes):
                dve(lambda e: e.scalar_tensor_tensor(out=out, in0=in0, scalar=scalar, in1=in1, op0=op0, op1=op1),
                    reads, writes)

            def V_copy(out, in_, reads, writes):
                dve(lambda e: e.tensor_copy(out=out, in_=in_), reads, writes)

            def V_recip(out, in_, reads, writes):
                dve(lambda e: e.reciprocal(out=out, in_=in_), reads, writes)

            _tmpi = [0]

            def newtmp():
                i = _tmpi[0] % NTMP
                _tmpi[0] += 1
                return i

            _pti = [0]

            def newpt():
                i = _pti[0] % NPT
                _pti[0] += 1
                return i

            def w_issue(P_, spec, slot):
                kind = spec[0]
                if kind == "bf":
                    _, name, li, r0, nrc, c0, ncols = spec
                    t, R, C = WB[name]
                    src = AP(t, li * R * C + r0 * C + c0, [[C, 128], [128 * C, nrc], [1, ncols]])
                    dst = wbuf[:, slot, 0:nrc, 0:ncols]
                    rd = [("wb", name, li)]
                else:
                    _, l, c0 = spec
                    src = AP(Dm["ada_w"], l * D * 6 * D + c0, [[6 * D, 128], [128 * 6 * D, 8], [1, 256]])
                    dst = wbuf[:, slot, :, :].bitcast(F32)
                    rd = []
                P_.op("sp", lambda e: e.dma_start(out=dst, in_=src), rd, [("w", slot)], key=f"w{slot}")

            def wget(spec, live=1):
                return WS.get(P, spec, w_issue, live=live)

            def kv_issue(P_, spec, slot):
                li, g, h, m, c = spec
                nk = min(1024, (g + 1) * GT - c * 1024)
                srck = AP(Kc, (h * 128 + m * 64) * S + c * 1024, [[S, 64], [1, nk]])
                srcv = AP(Vc, h * 128 * NBLK * 128 + c * 8 * 128, [[NBLK * 128, 128], [1, nk]])
                rd = [("Kc", gg) for gg in range(2 * c, min(2 * c + 2, g + 1))]
                rdv = [("Vc", gg) for gg in range(2 * c, min(2 * c + 2, g + 1))]
                dk = kbuf[m * 64:(m + 1) * 64, slot, 0:nk]
                dv = vbuf[:, slot, :, :].rearrange("p a b -> p (a b)")[:, 0:nk]
                P_.op("sp", lambda e: e.dma_start(out=dk, in_=srck), rd, [("kb", slot)], key=f"kv{slot}")
                P_.op("sp", lambda e: e.dma_start(out=dv, in_=srcv), rdv, [("vb", slot)], key=f"kv{slot}")

            def kvget(spec):
                return KS.get(P, spec, kv_issue, ok=lambda a, b: a[0:2] == b[0:2], live=2)

            dma("sp", identf[:], Dm["ident"].ap(), [], [("identf",)], "c0")
            dma("sp", Jt[:], Dm["Jm"].ap(), [], [("Jt",)], "c0")
            dma("sp", oht[:, 0, :], Dm["oh_da"].ap(), [], [("oht",)], "c0")
            dma("sp", oht[:, 1, :], Dm["oh_sw"].ap(), [], [("oht",)], "c0")
            dma("sp", sm("cact", 0, 8), Dm["cT"].ap(), [], [("cact",)], "c0")
            dve(lambda e: e.memset(relb[:], 1.0), [], [("relb",)])
            dma("sp", relb[0:32, :], Dm["rel_bias"].ap(), [("relb",)], [("relb",)], "c0")
            V_copy(identb[:], identf[:], [("identf",)], [("identb",)])
            dve(lambda e: e.memset(ones_bf[:], 1.0), [], [("ones_bf",)])
            dve(lambda e: e.memset(ones_f[:], 1.0), [], [("ones_f",)])
            dve(lambda e: e.memset(sm("eps"), EPS), [], [("eps",)])
            dve(lambda e: e.memset(sm("zero"), 0.0), [], [("zero",)])

            stop_at(0)
            def conv(name, li, l):
                t, R, C = WB[name]
                n = R * C
                rows = n // 1024
                step = 2048
                r = 0
                while r < rows:
                    nr = min(step, rows - r)
                    src = AP(Dm[name], l * n + r * 1024, [[1024, nr], [1, 1024]])
                    dst = AP(t, li * n + r * 1024, [[1024, nr], [1, 1024]])
                    dma("pool", dst, src, [], [("wb", name, li)], f"cv{li}{name}")
                    r += nr

            def conv_ksd(li, l):
                t, R, C = WB["w_ksd"]
                for kv in range(2):
                    for rpt in range(2):
                        src = AP(Dm["w_in"], l * D * IN_COLS + C_KS + kv * 64, [[IN_COLS, D], [1, 64]])
                        dst = AP(t, li * D * 256 + (2 * kv + rpt) * 64, [[256, D], [1, 64]])
                        dma("pool", dst, src, [], [("wb", "w_ksd", li)], f"cv{li}w_ksd")

            def conv_layer(li, l):
                conv("w_in", li, l)
                conv_ksd(li, l)
                for name in ("w_pa", "w_pb", "w_o", "w_ffn_in", "w_ffn_out"):
                    conv(name, li, l)

            conv_layer(0, LAY[0])

            stop_at(1)
            A_act(sm("cneg", 0, 8), sm("cact", 0, 8), AF.Exp, [("cact",)], [("cneg",)], scale=-1.0)
            V_ts(sm("cneg", 0, 8), sm("cneg", 0, 8), 1.0, None, ALU.add, None, [("cneg",)], [("cneg",)])
            V_recip(sm("cneg", 0, 8), sm("cneg", 0, 8), [("cneg",)], [("cneg",)])
            V_tt(sm("cact", 0, 8), sm("cact", 0, 8), sm("cneg", 0, 8), ALU.mult, [("cact",), ("cneg",)], [("cact",)])

            mm(PSB[0][0:8, 0:384], relb[:, 0:8], oht[:, 0, :], True, True,
               [("relb",), ("oht",)], [("ps", 0)])
            V_copy(Fsb[0:8, :], PSB[0][0:8, 0:384], [("ps", 0)], [("Fsb", 0)])
            mm(PSB[1][0:16, 0:384], relb[:, 8:24], oht[:, 1, :], True, True,
               [("relb",), ("oht",)], [("ps", 1)])
            V_copy(tmp[0:16, 0, 0:384], PSB[1][0:16, 0:384], [("ps", 1)], [("tmp", 0)])
            dma("sp", AP(Fd, 0, [[384, 8], [1, 384]]), Fsb[0:8, :], [("Fsb", 0)], [("Fd",)], "fd")
            dma("sp", AP(Fd, 8 * 384, [[384, 16], [1, 384]]), tmp[0:16, 0, 0:384], [("tmp", 0)], [("Fd",)], "fd")
            _tmpi[0] = 1
            for hp in range(12):
                ti = newtmp()
                Hs = tmp[:, ti, :]
                for k in range(2):
                    h = 2 * hp + k
                    dma("sp", Hs[:, k * 256:(k + 1) * 256], AP(Fd, h * 384, [[1, 128], [1, 256]]),
                        [("Fd",)], [("tmp", ti)], f"hk{ti}")
                b = hp % 2
                mm(PSB[b][:, :], Jt[:], Hs, True, True, [("Jt",), ("tmp", ti)], [("ps", b)])
                V_copy(Tt[:, 2 * hp:2 * hp + 2, :].rearrange("p a b -> p (a b)"), PSB[b][:, :],
                       [("ps", b)], [("Tt", 2 * hp), ("Tt", 2 * hp + 1)])

            stop_at(2)
            bankrot = [0]

            def nbank(pool):
                b = pool[bankrot[0] % len(pool)]
                bankrot[0] += 1
                return b

            for li, l in enumerate(LAY):
                lam_init = 0.8 - 0.6 * math.exp(-0.3 * (l + cfg.lam_layer))
                dve(lambda e: e.memset(carry[:], 0.0), [], [("carry", j) for j in range(NFF)])
                dma("sp", sm("adab", 0, 48), AP(Dm["adabT"], l * 128 * 48, [[48, 128], [1, 48]]), [], [("adab",)], "lp")
                dma("sp", sm("normg", 0, 16), AP(Dm["normg"], l * 128 * 16, [[16, 128], [1, 16]]), [], [("normg",)], "lp")
                dma("sp", lamt[:], AP(Dm["lamv"], l * 128 * 256, [[256, 128], [1, 256]]), [], [("lamt",)], "lp")
                dma("sp", sm("subg"), AP(Dm["sublnT"], l * 128, [[1, 128], [1, 1]]), [], [("subg",)], "lp")
                dma("sp", sm("sinks", 0, 8), AP(Dm["sinksT"], l * 128 * 8, [[8, 128], [1, 8]]), [], [("sinks",)], "lp")
                dma("sp", convp[:], AP(Dm["convp"], l * 128 * NFF * 4, [[NFF * 4, 128], [1, NFF * 4]]), [], [("convp",)], "lp")
                dma("sp", adabrow[:, 0, :], AP(Dm["ada_b"], l * 6 * D + 2 * D, [[D, 1], [1, D]]), [], [("xg", 0), ("xg", 1)], "lp")
                dma("sp", adabrow[:, 1, :], AP(Dm["ada_b"], l * 6 * D + 5 * D, [[D, 1], [1, D]]), [], [("xg", 0), ("xg", 1)], "lp")
                lt = lamt[:].rearrange("p (a b) -> p a b", a=4)
                V_tt(lamt[:, 0:64], lamt[:, 0:64], lamt[:, 64:128], ALU.mult, [("lamt",)], [("lamt",)])
                V_tt(lamt[:, 128:192], lamt[:, 128:192], lamt[:, 192:256], ALU.mult, [("lamt",)], [("lamt",)])
                dve(lambda e: e.reduce_sum(out=sm("lam4", 0, 1), in_=lamt[:, 0:64], axis=mybir.AxisListType.X),
                    [("lamt",)], [("lam4",)])
                dve(lambda e: e.reduce_sum(out=sm("lam4", 1, 1), in_=lamt[:, 128:192], axis=mybir.AxisListType.X),
                    [("lamt",)], [("lam4",)])
                A_act(sm("lam4", 2, 2), sm("lam4", 0, 2), AF.Exp, [("lam4",)], [("lam4",)])
                V_tt(sm("nlam"), sm("lam4", 3, 1), sm("lam4", 2, 1), ALU.subtract, [("lam4",)], [("nlam",)])
                V_ts(sm("nlam"), sm("nlam"), -lam_init, None, ALU.add, None, [("nlam",)], [("nlam",)])
                V_ts(sm("sg"), sm("subg"), 1.0 - lam_init, None, ALU.mult, None, [("subg",)], [("sg",)])
                A_act(sm("esink", 0, 8), sm("sinks", 0, 8), AF.Exp, [("sinks",)], [("esink",)])

                cbc = tmp[:, 4:6, :].rearrange("p a (b c) -> p (a b) c", c=128)
                for kc in range(8):
                    V_ts(cbc[:, kc, :], ones_f[:], sm("cact", kc, 1), None, ALU.mult, None,
                         [("ones_f",), ("cact",)], [("tmp", 4), ("tmp", 5)])
                MB = 7
                for sl in range(24):
                    c0 = sl * 256
                    slot = wget(("ada", l, c0))
                    wv = wbuf[:, slot, :, :].bitcast(F32)
                    sec = c0 // D
                    if sec in (2, 5):
                        which = 0 if sec == 2 else 1
                        cc0 = c0 - sec * D
                        b = nbank([4, 5])
                        for kc in range(8):
                            mm(PSB[b][:, 0:256], cbc[:, kc, :], wv[:, kc, :], kc == 0, False,
                               [("tmp", 4), ("tmp", 5), ("w", slot)], [("ps", b)])
                        mm(PSB[b][:, 0:256], ones_f[0:1, :], adabrow[0:1, which, cc0:cc0 + 256], False, True,
                           [("ones_f",), ("xg", 0), ("xg", 1)], [("ps", b)])
                        V_copy(gbc[:, which, cc0:cc0 + 256], PSB[b][:, 0:256], [("ps", b)], [("gbc", which)])
                    else:
                        mcol0 = {0: 0, 1: 8, 3: 16, 4: 24}[sec] + (c0 - sec * D) // 128
                        for ci in range(2):
                            for kc in range(8):
                                mm(PSB[MB][:, mcol0 + ci:mcol0 + ci + 1], wv[:, kc, ci * 128:(ci + 1) * 128],
                                   sm("cact", kc, 1), kc == 0, kc == 7,
                                   [("w", slot), ("cact",)], [("ps", MB)])
                adv = small[:, SM["adab"][0]:SM["adab"][0] + 48]
                for (mo, ao) in ((0, 0), (8, 8), (16, 24), (24, 32)):
                    V_tt(sm("modT", mo, 8), PSB[MB][:, mo:mo + 8], adv[:, ao:ao + 8], ALU.add,
                         [("ps", MB), ("adab",)], [("modT",)])
                V_stt(sm("A1", 0, 8), sm("modT", 8, 8), 1.0, sm("normg", 0, 8), ALU.add, ALU.mult,
                      [("modT",), ("normg",)], [("A1",)])
                V_stt(sm("A2", 0, 8), sm("modT", 24, 8), 1.0, sm("normg", 8, 8), ALU.add, ALU.mult,
                      [("modT",), ("normg",)], [("A2",)])
                for nm_, an_, bo_ in (("AB1", "A1", 0), ("AB2", "A2", 16)):
                    abv = sm(nm_, 0, 16).rearrange("p (k t) -> p k t", t=2)
                    V_copy(abv[:, :, 0], sm(an_, 0, 8), [(an_,)], [(nm_,)])
                    V_copy(abv[:, :, 1], sm("modT", bo_, 8), [("modT",), (nm_,)], [(nm_,)])

                stop_at(3)
                def norm_stage(Aname, Boff):
                    for t in range(4):
                        xt = xg[:, t, :]
                        yb = yf[:, t % 2, :]
                        yr = ("yf", t % 2)
                        stop_at(3.1)
                        A_act(yb, xt, AF.Square, [("xg", t)], [yr, ("ss", t)], accum=sm("ss", t, 1))
                        stop_at(3.2)
                        A_act(sm("lnv", t, 1), sm("ss", t, 1), AF.Ln, [("ss", t), ("eps",)], [("lnv", t)],
                              scale=1.0 / D, bias=sm("eps"))
                        A_act(sm("rstd", t, 1), sm("lnv", t, 1), AF.Exp, [("lnv", t)], [("rstd", t)], scale=-0.5)
                        stop_at(3.3)
                        V_ts(yb, xt, sm("rstd", t, 1), None, ALU.mult, None, [("xg", t), ("rstd", t)], [yr])
                        stop_at(3.4)
                        bb = (4, 5) if t % 2 == 0 else (6, 7)
                        for kc in range(8):
                            b = bb[kc // 4]
                            pe(lambda e, kc=kc, b=b, yb=yb: e.transpose(
                                out=PSB[b][:, (kc % 4) * 128:(kc % 4 + 1) * 128], in_=yb[:, kc * 128:(kc + 1) * 128],
                                identity=identf[:]),
                               [yr, ("identf",)], [("ps", b)])
                        stop_at(3.5)
                        for kc in range(8):
                            b = bb[kc // 4]
                            o_ = hT[:, kc, t * 128:(t + 1) * 128]
                            i_ = PSB[b][:, (kc % 4) * 128:(kc % 4 + 1) * 128]
                            if True:
                                A_act(o_, i_, AF.Identity, [("ps", b), (Aname,)], [("hT", kc, t)],
                                      scale=sm(Aname, 2 * kc, 1), bias=sm(Aname, 2 * kc + 1, 1))
                            else:
                                V_ts(o_, i_, sm(Aname, 2 * kc, 1), sm(Aname, 2 * kc + 1, 1), ALU.mult, ALU.add,
                                     [("ps", b), (Aname,)], [("hT", kc, t)])
                            stop_at(3.51 + kc * 0.01)
                        stop_at(3.6 + t * 0.01)

                def fm_proj(slot, ci, bank, rhs3, rres, nk=8):
                    for kc in range(nk):
                        mm(PSB[bank][:, :], wbuf[:, slot, kc, ci * 128:(ci + 1) * 128], rhs3[:, kc, :],
                           kc == 0, kc == nk - 1, [("w", slot)] + rres(kc), [("ps", bank)])

                hres = lambda kc: [("hT", kc, t_) for t_ in range(4)]
                evac_i = [0]

                def evac(out, in_, reads, writes):
                    evac_i[0] += 1
                    if evac_i[0] % 2:
                        A_act(out, in_, AF.Copy, reads, writes)
                    else:
                        V_copy(out, in_, reads, writes)

                def kst(h):
                    return (kstage[:, h, :], ("big", 16 + h)) if h < 6 else (kst2[:, h - 6, :], ("kst2", h - 6))

                for g in range(NG):
                    xsrc = Dm["x"] if li == 0 else out_t
                    xread = [] if li == 0 else [("xd", g * 4 + t) for t in range(4)]
                    for t in range(4):
                        dma("sp", xg[:, t, :], AP(xsrc, (g * 4 + t) * 128 * D, [[D, 128], [1, D]]),
                            [("xd", g * 4 + t)] if li else [], [("xg", t)], f"xg{t}")
                    norm_stage("AB1", 0)
                    stop_at(4)
                    gs = g % 2
                    DP = [0, 1, 2, 3, 4, 5]
                    for s2 in range(2):
                        slot = wget(("bf", "w_in", li, 0, 8, C_VA + s2 * 512, 512))
                        for t in range(4):
                            b = nbank(DP)
                            for kc in range(8):
                                mm(PSB[b][:, :], hT[:, kc, t * 128:(t + 1) * 128], wbuf[:, slot, kc, :],
                                   kc == 0, kc == 7, [("w", slot), ("hT", kc, t)], [("ps", b)])
                            ov = vstage[:, s2 * 4:s2 * 4 + 4, t * 128:(t + 1) * 128]
                            iv = PSB[b][:, :].rearrange("p (a b) -> p a b", a=4)
                            evac(ov, iv, [("ps", b)], [("vm", s2 * 4 + k) for k in range(4)])
                    dma("pool", AP(Vc, g * 4 * 128, [[NBLK * 128, 128], [128 * NBLK * 128, 8], [1, GT]]), vstage[:],
                        [("vm", k) for k in range(8)], [("Vc", g)], "vst")
                    if g == 0 and li + 1 < nlay:
                        conv_layer(li + 1, LAY[li + 1])
                    slot = wget(("bf", "w_in", li, 0, 8, C_VS, 128))
                    for t in range(4):
                        b = nbank(DP)
                        for kc in range(8):
                            mm(PSB[b][:, 0:128], hT[:, kc, t * 128:(t + 1) * 128], wbuf[:, slot, kc, 0:128],
                               kc == 0, kc == 7, [("w", slot), ("hT", kc, t)], [("ps", b)])
                        evac(vsw[:, gs, t, :], PSB[b][:, 0:128], [("ps", b)], [("vsw", gs, t)])

                    for s2 in range(2):
                        slot = wget(("bf", "w_in", li, 0, 8, C_QA + s2 * 512, 512))
                        for ci in range(4):
                            b = nbank(DP)
                            fm_proj(slot, ci, b, hT, hres)
                            oc = s2 * 4 + ci
                            evac(qT[:, oc, :], PSB[b][:, :], [("ps", b)], [("big", oc)])
                    for s2 in range(2):
                        slot = wget(("bf", "w_in", li, 0, 8, C_KA + s2 * 512, 512))
                        for ci in range(4):
                            b = nbank(DP)
                            fm_proj(slot, ci, b, hT, hres)
                            ko, kr = kst(s2 * 4 + ci)
                            evac(ko, PSB[b][:, :], [("ps", b)], [kr])
                    dma("pool", AP(Kc, g * GT, [[S, 128], [128 * S, 6], [1, GT]]), kstage,
                        [("big", 16 + h) for h in range(6)], [("Kc", g)], "kst")
                    dma("pool", AP(Kc, 6 * 128 * S + g * GT, [[S, 128], [128 * S, 2], [1, GT]]), kst2[:],
                        [("kst2", 0), ("kst2", 1)], [("Kc", g)], "kst")
                    for s2 in range(2):
                        slot = wget(("bf", "w_in", li, 0, 8, C_QS + s2 * 512, 512))
                        for ci in range(4):
                            b = nbank(DP)
                            fm_proj(slot, ci, b, hT, hres)
                            oc = s2 * 4 + ci
                            evac(qsT[:, oc, :], PSB[b][:, :], [("ps", b)], [("big", 8 + oc)])
                    slot = wget(("bf", "w_ksd", li, 0, 8, 0, 256))
                    for kv in range(2):
                        b = nbank(DP)
                        fm_proj(slot, kv, b, hT, hres)
                        evac(ksw[:, gs, kv, :], PSB[b][:, :], [("ps", b)], [("ksw", gs, kv)])
                    stop_at(5)
                    pend = []
                    SKEW = 3

                    def push(fn):
                        pend.append(fn)
                        while len(pend) > SKEW:
                            pend.pop(0)()

                    def flush():
                        while pend:
                            pend.pop(0)()

                    accsel = [0]
                    SB_ = [0, 1, 2, 7]
                    srot = [0]

                    for ch in range(8):
                        ab = accsel[0] % 2
                        accsel[0] += 1
                        OB, UB = 3 + 2 * ab, 4 + 2 * ab
                        for half in range(2):
                            j = 2 * ch + half
                            kv = j // 8
                            p0 = half * 64
                            first = [True]
                            for jj in range(4 * g - 1, 4 * g + 4):
                                if jj < 0:
                                    continue
                                qb0, qb1 = max(jj, 4 * g), min(jj + 1, 4 * g + 3)
                                c0, c1 = (qb0 - 4 * g) * 128, (qb1 - 4 * g + 1) * 128
                                N = c1 - c0
                                if jj == 4 * g - 1:
                                    ks_, kc0, vs_, vt = 1 - gs, 384, 1 - gs, 3
                                else:
                                    ks_, kc0, vs_, vt = gs, (jj - 4 * g) * 128, gs, jj - 4 * g
                                tc0 = 0 if qb0 == jj else 128
                                sbk = SB_[srot[0] % 4]
                                srot[0] += 1
                                mm(PSB[sbk][:, 0:N], ksw[p0:p0 + 64, ks_, kv, kc0:kc0 + 128],
                                   qsT[p0:p0 + 64, ch, c0:c1], True, True,
                                   [("ksw", ks_, kv), ("big", 8 + ch)], [("ps", sbk)])
                                ti = newtmp()
                                V_stt(tmp[:, ti, 0:N], PSB[sbk][:, 0:N], SCALE, Tt[:, 8 + j, tc0:tc0 + N],
                                      ALU.mult, ALU.add, [("ps", sbk), ("Tt", 8 + j)], [("tmp", ti)])
                                pi = newpt()
                                A_act(pT[:, pi, 0:N], tmp[:, ti, 0:N], AF.Exp, [("tmp", ti)], [("pT", pi)])

                                def pv(pi=pi, N=N, c0=c0, c1=c1, vs_=vs_, vt=vt, kv=kv, p0=p0, st=first[0],
                                       OB=OB, UB=UB):
                                    mm(PSB[OB][p0:p0 + 64, c0:c1], vsw[:, vs_, vt, kv * 64:(kv + 1) * 64],
                                       pT[:, pi, 0:N], st, False, [("vsw", vs_, vt), ("pT", pi)], [("ps", OB)], skip=True)
                                    mm(PSB[UB][p0:p0 + 64, c0:c1], ones_bf[:, 0:64],
                                       pT[:, pi, 0:N], st, False, [("ones_bf",), ("pT", pi)], [("ps", UB)], skip=True)
                                push(pv)
                                first[0] = False

                        def fin_sw(ch=ch, OB=OB, UB=UB):
                            ti = newtmp()
                            V_ts(tmp[:, ti, :], PSB[UB][:, :], sm("esink", ch, 1), None, ALU.add, None,
                                 [("ps", UB), ("esink",)], [("tmp", ti)])
                            V_recip(tmp[:, ti, :], tmp[:, ti, :], [("tmp", ti)], [("tmp", ti)])
                            V_tt(ybT[:, ch, :], PSB[OB][:, :], tmp[:, ti, :], ALU.mult,
                                 [("ps", OB), ("tmp", ti)], [("ybT", ch)])
                        push(fin_sw)

                    flush()
                    stop_at(6)
                    nkb = 4 * (g + 1)
                    nch = (nkb + 7) // 8
                    for h in range(8):
                        cb = Tt[:, h, 255:256]
                        hold = [None]
                        for m in range(2):
                            ab = accsel[0] % 2
                            accsel[0] += 1
                            OB, UB = 3 + 2 * ab, 4 + 2 * ab
                            p0 = m * 64
                            for kb in range(nkb):
                                kk = kb % 8
                                if kk == 0:
                                    slot = kvget((li, g, h, m, kb // 8))
                                qb0 = max(kb, 4 * g)
                                c0 = (qb0 - 4 * g) * 128
                                N = GT - c0
                                sbk = SB_[srot[0] % 4]
                                srot[0] += 1
                                mm(PSB[sbk][:, 0:N], kbuf[p0:p0 + 64, slot, kk * 128:(kk + 1) * 128],
                                   qT[p0:p0 + 64, h, c0:GT], True, True,
                                   [("kb", slot), ("big", h)], [("ps", sbk)])
                                pi = newpt()
                                if kb < 4 * g - 1:
                                    A_act(pT[:, pi, 0:N], PSB[sbk][:, 0:N], AF.Exp, [("ps", sbk), ("Tt", h)],
                                          [("pT", pi)], scale=SCALE, bias=cb)
                                else:
                                    if kb == 4 * g - 1:
                                        nb, tc0 = 128, 128
                                    else:
                                        nb, tc0 = min(256, N), 0
                                    ti = newtmp()
                                    V_stt(tmp[:, ti, 0:nb], PSB[sbk][:, 0:nb], SCALE, Tt[:, h, tc0:tc0 + nb],
                                          ALU.mult, ALU.add, [("ps", sbk), ("Tt", h)], [("tmp", ti)])
                                    A_act(pT[:, pi, 0:nb], tmp[:, ti, 0:nb], AF.Exp, [("tmp", ti)], [("pT", pi)])
                                    if N > nb:
                                        A_act(pT[:, pi, nb:N], PSB[sbk][:, nb:N], AF.Exp, [("ps", sbk), ("Tt", h)],
                                              [("pT", pi)], scale=SCALE, bias=cb)

                                def pv(pi=pi, N=N, c0=c0, slot=slot, kk=kk, st=(kb == 0), OB=OB, UB=UB):
                                    mm(PSB[OB][:, c0:GT], vbuf[:, slot, kk, :], pT[:, pi, 0:N], st, False,
                                       [("vb", slot), ("pT", pi)], [("ps", OB)], skip=True)
                                    mm(PSB[UB][:, c0:GT], ones_bf[:], pT[:, pi, 0:N], st, False,
                                       [("ones_bf",), ("pT", pi)], [("ps", UB)], skip=True)
                                push(pv)

                            def fin_da(m=m, h=h, OB=OB, UB=UB, hold=hold):
                                tr = newtmp()
                                V_recip(tmp[:, tr, :], PSB[UB][:, :], [("ps", UB)], [("tmp", tr)])
                                o0 = o0buf[:, h % 2, :]
                                o0r = ("o0", h % 2)
                                if m == 0:
                                    V_tt(o0, PSB[OB][:, :], tmp[:, tr, :], ALU.mult,
                                         [("ps", OB), ("tmp", tr)], [o0r])
                                else:
                                    V_tt(tmp[:, tr, :], PSB[OB][:, :], tmp[:, tr, :], ALU.mult,
                                         [("ps", OB), ("tmp", tr)], [("tmp", tr)])
                                    V_stt(o0, tmp[:, tr, :], sm("nlam"), o0, ALU.mult, ALU.add,
                                          [("tmp", tr), o0r, ("nlam",)], [o0r])
                                    A_act(tmp[:, tr, :], o0, AF.Square, [o0r], [("tmp", tr)])
                                    stb = SB_[srot[0] % 4]
                                    srot[0] += 1
                                    mm(PSB[stb][:, :], ones_f[:], tmp[:, tr, :], True, True,
                                       [("ones_f",), ("tmp", tr)], [("ps", stb)])
                                    A_act(tmp[:, tr, :], PSB[stb][:, :], AF.Ln, [("ps", stb), ("eps",)], [("tmp", tr)],
                                          scale=1.0 / 128, bias=sm("eps"))
                                    A_act(tmp[:, tr, :], tmp[:, tr, :], AF.Exp, [("tmp", tr)], [("tmp", tr)], scale=-0.5)
                                    V_stt(yaT[:, h, :], o0, sm("sg"), tmp[:, tr, :], ALU.mult, ALU.mult,
                                          [o0r, ("tmp", tr), ("sg",)], [("yaT", h)])
                            push(fin_da)
                    flush()
                    if "att" in cfg.debug and g == cfg.NG - 1 and li == 0:
                        for nm_, t_, rs_ in (("dbg_yaT", yaT, [("yaT", k) for k in range(8)]),
                                             ("dbg_ybT", ybT, [("ybT", k) for k in range(8)]),
                                             ("dbg_hT", hT, [("hT", k, t_) for k in range(8) for t_ in range(4)])):
                            dt_ = dbg_out[nm_]
                            dma("pool", dt_.ap(), t_[:], rs_, [("dbg", nm_)], "dbg")

                    stop_at(7)
                    for s2 in range(2):
                        slot = wget(("bf", "w_pa", li, 0, 8, s2 * 512, 512))
                        for ci in range(4):
                            fm_proj(slot, ci, ci, yaT, lambda kc: [("yaT", kc)])
                        slot = wget(("bf", "w_in", li, 0, 8, C_GA + s2 * 512, 512))
                        for ci in range(4):
                            fm_proj(slot, ci, 4 + ci, hT, hres)
                        for ci in range(4):
                            A_act(tmp[:, ci, :], PSB[4 + ci][:, :], AF.Sigmoid, [("ps", 4 + ci)], [("tmp", ci)])
                            V_tt(tmp[:, ci, :], PSB[ci][:, :], tmp[:, ci, :], ALU.mult,
                                 [("ps", ci), ("tmp", ci)], [("tmp", ci)])
                        slot = wget(("bf", "w_pb", li, 0, 8, s2 * 512, 512))
                        for ci in range(4):
                            fm_proj(slot, ci, ci, ybT, lambda kc: [("ybT", kc)])
                        slot = wget(("bf", "w_in", li, 0, 8, C_GB + s2 * 512, 512))
                        for ci in range(4):
                            fm_proj(slot, ci, 4 + ci, hT, hres)
                        for ci in range(4):
                            tb = 4 + (ci % 2)
                            A_act(tmp[:, tb, :], PSB[4 + ci][:, :], AF.Sigmoid, [("ps", 4 + ci)], [("tmp", tb)])
                            V_tt(tmp[:, tb, :], PSB[ci][:, :], tmp[:, tb, :], ALU.mult,
                                 [("ps", ci), ("tmp", tb)], [("tmp", tb)])
                            V_tt(mergedT[:, s2 * 4 + ci, :], tmp[:, ci, :], tmp[:, tb, :], ALU.add,
                                 [("tmp", ci), ("tmp", tb)], [("vm", s2 * 4 + ci)])
                    dbgon = "att" in cfg.debug and g == cfg.NG - 1 and li == 0
                    if dbgon:
                        dma("pool", dbg_out["dbg_mg"].ap(), mergedT[:], [("vm", k) for k in range(8)], [("dbg", "mg")], "dbg")
                        dma("pool", dbg_out["dbg_gbc"].ap(), gbc[:], [("gbc", 0), ("gbc", 1)], [("dbg", "gbc")], "dbg")
                    AP8 = [0, 1, 2, 3, 4, 5, 6, 7]
                    for chh in range(2):
                        slot = wget(("bf", "w_o", li, 0, 8, chh * 512, 512))
                        for t in range(4):
                            b = nbank(AP8)
                            for cc in range(8):
                                mm(PSB[b][:, :], mergedT[:, cc, t * 128:(t + 1) * 128], wbuf[:, slot, cc, :],
                                   cc == 0, cc == 7, [("vm", cc), ("w", slot)], [("ps", b)])
                            ti = newtmp()
                            V_tt(tmp[:, ti, :], PSB[b][:, :], gbc[:, 0, chh * 512:(chh + 1) * 512], ALU.mult,
                                 [("ps", b), ("gbc", 0)], [("tmp", ti)])
                            xs = xg[:, t, chh * 512:(chh + 1) * 512]
                            V_tt(xs, xs, tmp[:, ti, :], ALU.add, [("xg", t), ("tmp", ti)], [("xg", t)])
                    stop_at(8)
                    if dbgon:
                        dma("pool", dbg_out["dbg_x1"].ap(), xg[:], [("xg", k) for k in range(4)], [("dbg", "x1")], "dbg")
                    norm_stage("AB2", 16)
                    slotA = slotB = None
                    for j in range(NFF):
                        s2, ci = j // 4, j % 4
                        ncol = 512 if s2 < 5 else 256
                        if ci == 0:
                            slotA = wget(("bf", "w_ffn_in", li, 0, 8, s2 * 512, ncol))
                            slotB = wget(("bf", "w_ffn_in", li, 0, 8, D_FF + s2 * 512, ncol), live=2)
                        bA, bB = [(0, 1), (2, 3), (4, 5)][j % 3]
                        fm_proj(slotA, ci, bA, hT, hres)
                        fm_proj(slotB, ci, bB, hT, hres)
                        ab_ = abuf[:, j % 2, :]
                        abr = ("abuf", j % 2)
                        P.op("pool", lambda e, ab_=ab_, j=j: e.tensor_copy(out=ab_[:, 0:2], in_=carry[:, j, :]),
                             [("carry", j)], [abr])
                        A_act(ab_[:, 2:2 + GT], PSB[bA][:, :], AF.Copy, [("ps", bA)], [abr])
                        P.op("pool", lambda e, ab_=ab_, j=j: e.tensor_copy(out=carry[:, j, :], in_=ab_[:, GT:GT + 2]),
                             [abr], [("carry", j)])
                        cw = lambda k, j=j: convp[:, j * 4 + k:j * 4 + k + 1]
                        ti = newtmp()
                        A_act(tmp[:, ti, :], ab_[:, 2:2 + GT], AF.Identity, [abr, ("convp",)], [("tmp", ti)],
                              scale=cw(2), bias=cw(3))
                        V_stt(tmp[:, ti, :], ab_[:, 1:1 + GT], cw(1), tmp[:, ti, :], ALU.mult, ALU.add,
                              [abr, ("convp",), ("tmp", ti)], [("tmp", ti)])
                        V_stt(tmp[:, ti, :], ab_[:, 0:GT], cw(0), tmp[:, ti, :], ALU.mult, ALU.add,
                              [abr, ("convp",), ("tmp", ti)], [("tmp", ti)])
                        A_act(tmp[:, ti, :], tmp[:, ti, :], AF.Silu, [("tmp", ti)], [("tmp", ti)])
                        V_tt(uT[:, j, :], PSB[bB][:, :], tmp[:, ti, :], ALU.mult,
                             [("ps", bB), ("tmp", ti)], [("big", j)])
                    last_layer = (li == nlay - 1)
                    if dbgon:
                        dma("pool", dbg_out["dbg_u"].ap(), uT[:], [("big", k) for k in range(22)], [("dbg", "u")], "dbg")
                    for chh in range(2):
                        accb = [0, 1, 2, 3] if chh == 0 else [4, 5, 6, 7]
                        for s3 in range(3):
                            j0 = s3 * 8
                            nj = min(8, NFF - j0)
                            slot = wget(("bf", "w_ffn_out", li, j0 * 128, nj, chh * 512, 512))
                            for t in range(4):
                                for jj in range(nj):
                                    j = j0 + jj
                                    mm(PSB[accb[t]][:, :], uT[:, j, t * 128:(t + 1) * 128], wbuf[:, slot, jj, :],
                                       j == 0, j == NFF - 1, [("big", j), ("w", slot)], [("ps", accb[t])])
                        for t in range(4):
                            ti = newtmp()
                            V_tt(tmp[:, ti, :], PSB[accb[t]][:, :], gbc[:, 1, chh * 512:(chh + 1) * 512], ALU.mult,
                                 [("ps", accb[t]), ("gbc", 1)], [("tmp", ti)])
                            xs = xg[:, t, chh * 512:(chh + 1) * 512]
                            V_tt(xs, xs, tmp[:, ti, :], ALU.add, [("xg", t), ("tmp", ti)], [("xg", t)])
                    if dbgon:
                        dma("pool", dbg_out["dbg_x2"].ap(), xg[:], [("xg", k) for k in range(4)], [("dbg", "x2")], "dbg")
                    for t in range(4):
                        dst = AP(out_t, (g * 4 + t) * 128 * D, [[D, 128], [1, D]])
                        if last_layer and cfg.final_norm:
                            xt = xg[:, t, :]
                            A_act(yf[:, t % 2, :], xt, AF.Square, [("xg", t)], [("yf", t % 2), ("ss2", t)], accum=sm("ss2", t, 1))
                            A_act(sm("lnv2", t, 1), sm("ss2", t, 1), AF.Ln, [("ss2", t), ("eps",)], [("lnv2", t)],
                                  scale=1.0 / D, bias=sm("eps"))
                            A_act(sm("rstd2", t, 1), sm("lnv2", t, 1), AF.Exp, [("lnv2", t)], [("rstd2", t)], scale=-0.5)
                            V_stt(xt, xt, sm("rstd2", t, 1), fgbt[:], ALU.mult, ALU.mult,
                                  [("xg", t), ("rstd2", t), ("fgb",)], [("xg", t)])
                        dma("pool", dst, xg[:, t, :], [("xg", t)], [("xd", g * 4 + t)], f"xs{t}")
            P.op("pool", None, [("xd", i) for i in range(NBLK)] + [("dbg", k) for k in dbg_out], [])

        fgbt = sb("fgbt", [128, D], F32)

        WS1, KS1 = Stream("w", NW), Stream("kv", 4)
        try:
            emit_all(_NullProg(), WS1, KS1)
        except StopEmit:
            pass
        P = Prog()
        P.op("sp", lambda e: e.dma_start(out=fgbt[:], in_=Dm["fgb"].ap()), [], [("fgb",)], key="c0")
        WS2, KS2 = Stream("w", NW, WS1.specs), Stream("kv", 4, KS1.specs)
        try:
            emit_all(P, WS2, KS2)
        except StopEmit:
            pass
        P.final_wait("pool")
        P.finalize()
        semnames = P.sem_names()
        sems = {}
        for i, sname in enumerate(semnames):
            sems[sname] = es.enter_context(nc.semaphore(f"s{i}"))
        block = es.enter_context(nc.Block())

        @block.tensor
        def _(e):
            P.emit_engine("pe", e, sems)

        @block.scalar
        def _(e):
            P.emit_engine("act", e, sems)

        @block.vector
        def _(e):
            P.emit_engine("dve", e, sems)

        @block.gpsimd
        def _(e):
            P.emit_engine("pool", e, sems)

        @block.sync
        def _(e):
            P.emit_engine("sp", e, sems)
    return nc, sorted(dbg_out.keys())


def _host_layouts(inp, nl):
    f = np.float32
    L = nl
    oh_da, oh_sw = _onehots()
    shared = {
        "rel_bias": np.ascontiguousarray(inp["rel_bias"], f),
        "oh_da": oh_da, "oh_sw": oh_sw,
        "Jm": np.ascontiguousarray(np.eye(128, dtype=f)[::-1]),
        "ident": np.eye(128, dtype=f),
        "ada_w": np.ascontiguousarray(inp["ada_w"], f),
        "ada_b": np.ascontiguousarray(inp["ada_b"], f),
        "adabT": np.ascontiguousarray(np.asarray(inp["ada_b"], f).reshape(L, 48, 128).transpose(0, 2, 1)),
        "normg": np.ascontiguousarray(np.concatenate(
            [np.asarray(inp["norm_mix_g"], f).reshape(L, 8, 128).transpose(0, 2, 1),
             np.asarray(inp["norm_ffn_g"], f).reshape(L, 8, 128).transpose(0, 2, 1)], axis=2)),
        "lamv": np.ascontiguousarray(np.broadcast_to(np.concatenate(
            [np.asarray(inp[k], f) for k in ("lam_q1", "lam_k1", "lam_q2", "lam_k2")], axis=1)[:, None, :],
            (L, 128, 256))),
        "sublnT": np.ascontiguousarray(np.asarray(inp["subln_g"], f).reshape(L, 128, 1)),
        "sinksT": np.ascontiguousarray(np.repeat(np.asarray(inp["sinks"], f).reshape(L, 8, 2).transpose(0, 2, 1),
                                                 64, axis=1)),
        "convp": np.ascontiguousarray(np.concatenate(
            [np.asarray(inp["conv_w"], f).reshape(L, 3, NFF, 128).transpose(0, 3, 2, 1),
             np.asarray(inp["conv_b"], f).reshape(L, NFF, 128).transpose(0, 2, 1)[..., None]], axis=3
        ).reshape(L, 128, NFF * 4)),
        "fgb": np.ascontiguousarray(np.broadcast_to(np.asarray(inp["final_g"], f)[None, :], (128, D))),
    }
    for k in ("w_in", "w_pa", "w_pb", "w_o", "w_ffn_in", "w_ffn_out"):
        shared[k] = np.ascontiguousarray(inp[k], f)
    return shared


def _core_inputs(shared, x_b, c_b):
    m = dict(shared)
    m["x"] = np.ascontiguousarray(x_b, np.float32)
    m["cT"] = np.ascontiguousarray(np.asarray(c_b, np.float32).reshape(8, 128).T)
    return m


_NC_CACHE = {}


def _get_nc(S, layers, final_norm, debug=()):
    key = (S, tuple(layers), final_norm, tuple(debug))
    if key not in _NC_CACHE:
        _NC_CACHE[key] = build_nc(Cfg(S=S, layers=layers, final_norm=final_norm, debug=debug))
    return _NC_CACHE[key]


MODE = "fused"


def kernel(**inputs):
    x = np.asarray(inputs["x"], np.float32)
    c = np.asarray(inputs["c"], np.float32)
    B, S, _ = x.shape
    shared = _host_layouts(inputs, DEPTH)
    cores = list(range(B))
    if MODE == "fused":
        nc, _ = build_nc(Cfg(S=S, layers=range(DEPTH), final_norm=True))
        in_maps = [_core_inputs(shared, x[b], c[b]) for b in range(B)]
        res = run_bass_kernel_spmd(nc, in_maps, core_ids=cores)
        return np.stack([np.asarray(r["out"]) for r in res.results], axis=0).astype(np.float32)
    cur = [x[b] for b in range(B)]
    per_layer = ("ada_w", "ada_b", "norm_mix_g", "norm_ffn_g", "w_in", "lam_q1", "lam_k1", "lam_q2", "lam_k2",
                 "subln_g", "sinks", "w_pa", "w_pb", "w_o", "w_ffn_in", "conv_w", "conv_b", "w_ffn_out")
    for l in range(DEPTH):
        sub = dict(inputs)
        for k in per_layer:
            sub[k] = np.asarray(inputs[k])[l:l + 1]
        shared_l = _host_layouts(sub, 1)
        cfg = Cfg(S=S, layers=(0,), final_norm=(l == DEPTH - 1), nl_total=1)
        cfg.lam_layer = l
        nc, _ = build_nc(cfg)
        in_maps = [_core_inputs(shared_l, cur[b], c[b]) for b in range(B)]
        res = run_bass_kernel_spmd(nc, in_maps, core_ids=cores)
        cur = [np.asarray(r["out"]) for r in res.results]
    return np.stack(cur, axis=0).astype(np.float32)
```

```python
import math
import bisect
import numpy as np
import concourse.bass as bass
import concourse.mybir as mybir
from concourse.bass_utils import run_bass_kernel_spmd

F32 = mybir.dt.float32
BF16 = mybir.dt.bfloat16
AF = mybir.ActivationFunctionType
ALU = mybir.AluOpType

D = 1024
DEPTH = 4
HD = 64
NB_BUCKETS = 32
D_FF = 2816
NFF = D_FF // 128
IN_COLS = 6400
EPS = 1e-6
GT = 512
MASKV = -30000.0
SCALE = HD ** -0.5

C_QA, C_KA, C_VA, C_QS, C_KS, C_VS, C_GA, C_GB = 0, 1024, 2048, 3072, 4096, 4224, 4352, 5376


class _Op:
    __slots__ = ("eng", "fn", "deps", "key", "count", "signal", "idx", "waits")


class Prog:
    def __init__(self):
        self.ops = []
        self.lastw = {}
        self.readers = {}

    def op(self, eng, fn, reads=(), writes=(), key=None):
        idx = len(self.ops)
        deps = set()
        for r in reads:
            w = self.lastw.get(r)
            if w is not None:
                deps.add(w)
        for w_ in writes:
            w = self.lastw.get(w_)
            if w is not None:
                deps.add(w)
            rl = self.readers.get(w_)
            if rl:
                deps.update(rl)
        for r in reads:
            self.readers.setdefault(r, []).append(idx)
        for w_ in writes:
            self.lastw[w_] = idx
            self.readers[w_] = []
        o = _Op()
        o.eng, o.fn, o.deps, o.key, o.idx = eng, fn, deps, key, idx
        o.count, o.signal, o.waits = 0, False, None
        self.ops.append(o)
        return idx

    def final_wait(self, eng):
        last = {}
        for o in self.ops:
            if o.fn is None:
                continue
            last[("k", o.key) if o.key is not None else o.eng] = o.idx
        idx = len(self.ops)
        o = _Op()
        o.eng, o.fn, o.deps, o.key, o.idx = eng, None, set(last.values()), None, idx
        o.count, o.signal, o.waits = 0, False, None
        self.ops.append(o)

    def finalize(self):
        ops = self.ops
        for o in ops:
            for d in o.deps:
                od = ops[d]
                if od.key is not None:
                    continue
                if od.eng == "pe" and o.eng == "pe" and o.key is None:
                    continue
                od.signal = True
        cnt = {}
        self.keylist = {}
        for o in ops:
            if o.key is not None:
                c = cnt.get(("k", o.key), 0) + 16
                cnt[("k", o.key)] = c
                o.count = c
                self.keylist.setdefault(o.key, ([], []))
                self.keylist[o.key][0].append(o.idx)
                self.keylist[o.key][1].append(c)
            elif o.signal:
                c = cnt.get(o.eng, 0) + 1
                cnt[o.eng] = c
                o.count = c
        waited = {}
        for o in ops:
            need = {}
            for d in o.deps:
                od = ops[d]
                if od.key is not None:
                    idxs, cums = self.keylist[od.key]
                    p = bisect.bisect_left(idxs, o.idx) - 1
                    sem = ("k", od.key)
                    val = cums[p]
                else:
                    if od.eng == "pe" and o.eng == "pe" and o.key is None:
                        continue
                    sem = od.eng
                    val = od.count
                if val > need.get(sem, 0):
                    need[sem] = val
            ws = []
            for sem, val in need.items():
                if val > waited.get((o.eng, sem), 0):
                    waited[(o.eng, sem)] = val
                    ws.append((sem, val))
            o.waits = ws

    def sem_names(self):
        s = set()
        for o in self.ops:
            if o.key is not None:
                s.add(("k", o.key))
            elif o.signal:
                s.add(o.eng)
        return sorted(s, key=str)

    def emit_engine(self, eng, e, sems):
        for o in self.ops:
            if o.eng != eng:
                continue
            for sem, val in o.waits:
                e.wait_ge(sems[sem], val)
            if o.fn is None:
                continue
            ins = o.fn(e)
            if o.key is not None:
                ins.then_inc(sems[("k", o.key)], 16)
            elif o.signal:
                ins.then_inc(sems[o.eng], 1)


class Stream:
    def __init__(self, name, nslots, specs=None):
        self.name, self.nslots = name, nslots
        self.record = specs is None
        self.specs = [] if specs is None else specs
        self.cur = 0
        self.issued = 0

    def get(self, P, spec, issue_fn, ok=None, live=1):
        i = self.cur
        self.cur += 1
        if self.record:
            self.specs.append(spec)
            return i % self.nslots
        assert self.specs[i] == spec, (self.name, i, self.specs[i], spec)
        lim = min(len(self.specs), i + self.nslots - live + 1)
        while self.issued < lim:
            if self.issued > i and ok is not None and not ok(self.specs[self.issued], spec):
                break
            issue_fn(P, self.specs[self.issued], self.issued % self.nslots)
            self.issued += 1
        assert self.issued > i
        return i % self.nslots


class _NullProg:
    def op(self, *a, **k):
        return 0


class StopEmit(Exception):
    pass


def _bucket(n):
    n = np.maximum(n, 0)
    me = NB_BUCKETS // 2
    large = me + (np.log(np.maximum(n, 1).astype(np.float32) / me) / math.log(128 / me)
                  * (NB_BUCKETS - me)).astype(np.int32)
    large = np.minimum(large, NB_BUCKETS - 1)
    return np.where(n < me, n, large)


def _onehots():
    j = np.arange(384)
    n = j - 127
    b = _bucket(n)
    oh_da = np.zeros((33, 384), np.float32)
    oh_sw = np.zeros((33, 384), np.float32)
    for jj in range(384):
        if n[jj] >= 0:
            oh_da[b[jj], jj] = 1.0
        else:
            oh_da[32, jj] = MASKV
        if 0 <= n[jj] < 128:
            oh_sw[b[jj], jj] = 1.0
        else:
            oh_sw[32, jj] = MASKV
    return oh_da, oh_sw


class Cfg:
    def __init__(self, S=4096, layers=(0, 1, 2, 3), final_norm=True, nl_total=DEPTH, debug=()):
        self.S = S
        self.layers = tuple(layers)
        self.final_norm = final_norm
        self.NL = nl_total
        self.debug = tuple(debug)
        self.stop = 99
        self.lam_layer = 0
        self.NG = S // GT
        self.NBLK = S // 128


def build_nc(cfg):
    nc = bass.Bass("TRN2", target_bir_lowering=False)
    S, NL, NG, NBLK = cfg.S, cfg.NL, cfg.NG, cfg.NBLK
    LAY = cfg.layers
    nlay = len(LAY)

    def din(name, shape, dt=F32):
        return nc.dram_tensor(name, list(shape), dt, kind="ExternalInput")

    Dm = {}
    Dm["x"] = din("x", [S, D])
    Dm["cT"] = din("cT", [128, 8])
    Dm["rel_bias"] = din("rel_bias", [32, 24])
    Dm["oh_da"] = din("oh_da", [33, 384])
    Dm["oh_sw"] = din("oh_sw", [33, 384])
    Dm["Jm"] = din("Jm", [128, 128])
    Dm["ident"] = din("ident", [128, 128])
    Dm["ada_w"] = din("ada_w", [NL, D, 6 * D])
    Dm["ada_b"] = din("ada_b", [NL, 6 * D])
    Dm["adabT"] = din("adabT", [NL, 128, 48])
    Dm["normg"] = din("normg", [NL, 128, 16])
    Dm["lamv"] = din("lamv", [NL, 128, 256])
    Dm["sublnT"] = din("sublnT", [NL, 128, 1])
    Dm["sinksT"] = din("sinksT", [NL, 128, 8])
    Dm["convp"] = din("convp", [NL, 128, NFF * 4])
    Dm["fgb"] = din("fgb", [128, D])
    Dm["w_in"] = din("w_in", [NL, D, IN_COLS])
    Dm["w_pa"] = din("w_pa", [NL, D, D])
    Dm["w_pb"] = din("w_pb", [NL, D, D])
    Dm["w_o"] = din("w_o", [NL, D, D])
    Dm["w_ffn_in"] = din("w_ffn_in", [NL, D, 2 * D_FF])
    Dm["w_ffn_out"] = din("w_ffn_out", [NL, D_FF, D])
    out_t = nc.dram_tensor("out", [S, D], F32, kind="ExternalOutput")
    WB = {
        "w_in": (nc.dram_tensor("wb_in", [nlay, D, IN_COLS], BF16, kind="Internal"), D, IN_COLS),
        "w_pa": (nc.dram_tensor("wb_pa", [nlay, D, D], BF16, kind="Internal"), D, D),
        "w_pb": (nc.dram_tensor("wb_pb", [nlay, D, D], BF16, kind="Internal"), D, D),
        "w_o": (nc.dram_tensor("wb_o", [nlay, D, D], BF16, kind="Internal"), D, D),
        "w_ffn_in": (nc.dram_tensor("wb_fi", [nlay, D, 2 * D_FF], BF16, kind="Internal"), D, 2 * D_FF),
        "w_ffn_out": (nc.dram_tensor("wb_fo", [nlay, D_FF, D], BF16, kind="Internal"), D_FF, D),
        "w_ksd": (nc.dram_tensor("wb_ksd", [nlay, D, 256], BF16, kind="Internal"), D, 256),
    }
    Kc = nc.dram_tensor("Kc", [8, 128, S], BF16, kind="Internal")
    Vc = nc.dram_tensor("Vc", [8, 128, NBLK, 128], BF16, kind="Internal")
    Fd = nc.dram_tensor("Fd", [24, 384], F32, kind="Internal")
    dbg_out = {}
    if "att" in cfg.debug:
        for nm_ in ("dbg_yaT", "dbg_ybT", "dbg_hT", "dbg_mg"):
            dbg_out[nm_] = nc.dram_tensor(nm_, [128, 8, GT], BF16, kind="ExternalOutput")
        dbg_out["dbg_u"] = nc.dram_tensor("dbg_u", [128, 22, GT], BF16, kind="ExternalOutput")
        dbg_out["dbg_x1"] = nc.dram_tensor("dbg_x1", [128, 4, D], F32, kind="ExternalOutput")
        dbg_out["dbg_x2"] = nc.dram_tensor("dbg_x2", [128, 4, D], F32, kind="ExternalOutput")
        dbg_out["dbg_gbc"] = nc.dram_tensor("dbg_gbc", [128, 2, D], F32, kind="ExternalOutput")

    def AP(t, off, ap):
        return bass.AP(tensor=t, offset=off, ap=[list(a) for a in ap])

    from contextlib import ExitStack
    es = ExitStack()
    with es:
        def sb(name, shape, dt):
            return es.enter_context(nc.sbuf_tensor("sb_" + name, list(shape), dt))

        Tt = sb("Tt", [128, 24, 256], F32)
        xg = sb("xg", [128, 4, D], F32)
        hT = sb("hT", [128, 8, GT], BF16)
        big = sb("big", [128, 22, GT], BF16)
        qT = big[:, 0:8, :]
        qsT = big[:, 8:16, :]
        kstage = big[:, 16:22, :]
        kst2 = sb("kst2", [128, 2, GT], BF16)
        uT = big
        vm = sb("vm", [128, 8, GT], BF16)
        vstage = vm
        mergedT = vm
        kbuf = sb("kbuf", [128, 4, 1024], BF16)
        vbuf = sb("vbuf", [128, 4, 8, 128], BF16)
        ksw = sb("ksw", [128, 2, 2, GT], BF16)
        vsw = sb("vsw", [128, 2, 4, 128], BF16)
        yaT = sb("yaT", [128, 8, GT], BF16)
        ybT = sb("ybT", [128, 8, GT], BF16)
        NW = 4
        wbuf = sb("wbuf", [128, NW, 8, GT], BF16)
        NPT = 5
        pT = sb("pT", [128, NPT, GT], BF16)
        NTMP = 6
        tmp = sb("tmp", [128, NTMP, GT], F32)
        abuf = sb("abuf", [128, 2, GT + 4], F32)
        yf = sb("yf", [128, 2, D], F32)
        gbc = sb("gbc", [128, 2, D], F32)
        identb = sb("identb", [128, 128], BF16)
        identf = sb("identf", [128, 128], F32)
        Jt = sb("Jt", [128, 128], F32)
        ones_bf = sb("ones_bf", [128, 128], BF16)
        ones_f = sb("ones_f", [128, 128], F32)
        small = sb("small", [128, 256], F32)
        carry = sb("carry", [128, NFF, 2], F32)
        convp = sb("convp", [128, NFF * 4], F32)
        lamt = sb("lamt", [128, 256], F32)
        relb = sb("relb", [33, 24], F32)
        oht = sb("oht", [33, 2, 384], F32)
        Fsb = sb("Fsb", [24, 384], F32)
        o0buf = sb("o0buf", [128, 2, GT], F32)
        adabrow = xg[0:1, 0:2, :]

        SM = {}
        _c = [0]

        def smalloc(name, n):
            SM[name] = (_c[0], n)
            _c[0] += n
            assert _c[0] <= 256

        def sm(name, i=0, n=1):
            o, _n = SM[name]
            return small[:, o + i:o + i + n]

        for nm, n in (("cact", 8), ("cneg", 8), ("modT", 32), ("adab", 48), ("normg", 16), ("A1", 8), ("A2", 8),
                      ("ss", 4), ("lnv", 4), ("rstd", 4), ("eps", 1), ("zero", 1), ("lam4", 4),
                      ("nlam", 1), ("sg", 1), ("subg", 1), ("esink", 8), ("sinks", 8), ("ss2", 4), ("lnv2", 4),
                      ("rstd2", 4), ("linit", 1), ("AB1", 16), ("AB2", 16)):
            smalloc(nm, n)

        PSB = [es.enter_context(nc.psum_tensor(f"ps{i}", [128, 512], F32)) for i in range(8)]

        def emit_all(P, WS, KS):
            def stop_at(level):
                if cfg.stop <= level:
                    raise StopEmit()

            def act(fn, reads, writes):
                P.op("act", fn, reads, writes)

            def dve(fn, reads, writes):
                P.op("dve", fn, reads, writes)

            def pe(fn, reads, writes):
                P.op("pe", fn, reads, writes)

            def dma(q, out, in_, reads, writes, key, **kw):
                P.op(q, lambda e: e.dma_start(out=out, in_=in_, **kw), reads, writes, key=key)

            def mm(out, lhsT, rhs, start, stop, reads, writes, skip=False):
                pe(lambda e: e.matmul(out, lhsT=lhsT, rhs=rhs, start=start, stop=stop,
                                      skip_group_check=skip), reads, writes)

            def A_act(out, in_, func, reads, writes, scale=None, bias=None, accum=None):
                kw = {}
                if scale is not None:
                    kw["scale"] = scale
                if bias is not None:
                    kw["bias"] = bias
                if accum is not None:
                    kw["accum_out"] = accum
                act(lambda e: e.activation(out=out, in_=in_, func=func, **kw), reads, writes)

            def V_ts(out, in0, s1, s2, op0, op1, reads, writes):
                if op1 is None:
                    dve(lambda e: e.tensor_scalar(out=out, in0=in0, scalar1=s1, scalar2=None, op0=op0),
                        reads, writes)
                else:
                    dve(lambda e: e.tensor_scalar(out=out, in0=in0, scalar1=s1, scalar2=s2, op0=op0, op1=op1),
                        reads, writes)

            def V_tt(out, in0, in1, op, reads, writes):
                dve(lambda e: e.tensor_tensor(out=out, in0=in0, in1=in1, op=op), reads, writes)

            def V_stt(out, in0, scalar, in1, op0, op1, reads, writes):
                dve(lambda e: e.scalar_tensor_tensor(out=out, in0=in0, scalar=scalar, in1=in1, op0=op0, op1=op1),
                    reads, writes)

            def V_copy(out, in_, reads, writes):
                dve(lambda e: e.tensor_copy(out=out, in_=in_), reads, writes)

            def V_recip(out, in_, reads, writes):
                dve(lambda e: e.reciprocal(out=out, in_=in_), reads, writes)

            _tmpi = [0]

            def newtmp():
                i = _tmpi[0] % NTMP
                _tmpi[0] += 1
                return i

            _pti = [0]

            def newpt():
                i = _pti[0] % NPT
                _pti[0] += 1
                return i

            def w_issue(P_, spec, slot):
                kind = spec[0]
                if kind == "bf":
                    _, name, li, r0, nrc, c0, ncols = spec
                    t, R, C = WB[name]
                    src = AP(t, li * R * C + r0 * C + c0, [[C, 128], [128 * C, nrc], [1, ncols]])
                    dst = wbuf[:, slot, 0:nrc, 0:ncols]
                    rd = [("wb", name, li)]
                else:
                    _, l, c0 = spec
                    src = AP(Dm["ada_w"], l * D * 6 * D + c0, [[6 * D, 128], [128 * 6 * D, 8], [1, 256]])
                    dst = wbuf[:, slot, :, :].bitcast(F32)
                    rd = []
                P_.op("sp", lambda e: e.dma_start(out=dst, in_=src), rd, [("w", slot)], key=f"w{slot}")

            def wget(spec, live=1):
                return WS.get(P, spec, w_issue, live=live)

            def kv_issue(P_, spec, slot):
                li, g, h, m, c = spec
                nk = min(1024, (g + 1) * GT - c * 1024)
                srck = AP(Kc, (h * 128 + m * 64) * S + c * 1024, [[S, 64], [1, nk]])
                srcv = AP(Vc, h * 128 * NBLK * 128 + c * 8 * 128, [[NBLK * 128, 128], [1, nk]])
                rd = [("Kc", gg) for gg in range(2 * c, min(2 * c + 2, g + 1))]
                rdv = [("Vc", gg) for gg in range(2 * c, min(2 * c + 2, g + 1))]
                dk = kbuf[m * 64:(m + 1) * 64, slot, 0:nk]
                dv = vbuf[:, slot, :, :].rearrange("p a b -> p (a b)")[:, 0:nk]
                P_.op("sp", lambda e: e.dma_start(out=dk, in_=srck), rd, [("kb", slot)], key=f"kv{slot}")
                P_.op("sp", lambda e: e.dma_start(out=dv, in_=srcv), rdv, [("vb", slot)], key=f"kv{slot}")

            def kvget(spec):
                return KS.get(P, spec, kv_issue, ok=lambda a, b: a[0:2] == b[0:2], live=2)

            dma("sp", identf[:], Dm["ident"].ap(), [], [("identf",)], "c0")
            dma("sp", Jt[:], Dm["Jm"].ap(), [], [("Jt",)], "c0")
            dma("sp", oht[:, 0, :], Dm["oh_da"].ap(), [], [("oht",)], "c0")
            dma("sp", oht[:, 1, :], Dm["oh_sw"].ap(), [], [("oht",)], "c0")
            dma("sp", sm("cact", 0, 8), Dm["cT"].ap(), [], [("cact",)], "c0")
            dve(lambda e: e.memset(relb[:], 1.0), [], [("relb",)])
            dma("sp", relb[0:32, :], Dm["rel_bias"].ap(), [("relb",)], [("relb",)], "c0")
            V_copy(identb[:], identf[:], [("identf",)], [("identb",)])
            dve(lambda e: e.memset(ones_bf[:], 1.0), [], [("ones_bf",)])
            dve(lambda e: e.memset(ones_f[:], 1.0), [], [("ones_f",)])
            dve(lambda e: e.memset(sm("eps"), EPS), [], [("eps",)])
            dve(lambda e: e.memset(sm("zero"), 0.0), [], [("zero",)])

            stop_at(0)
            deferred = []
            defer_on = [False]
            per_saved = [0]

            def cdma(dst, src, res, key):
                if defer_on[0]:
                    deferred.append((dst, src, res, key))
                else:
                    dma("pool", dst, src, [], [res], key)

            def conv(name, li, l):
                t, R, C = WB[name]
                n = R * C
                rows = n // 1024
                step = 2048
                r = 0
                while r < rows:
                    nr = min(step, rows - r)
                    src = AP(Dm[name], l * n + r * 1024, [[1024, nr], [1, 1024]])
                    dst = AP(t, li * n + r * 1024, [[1024, nr], [1, 1024]])
                    cdma(dst, src, ("wb", name, li), f"cv{li}{name}")
                    r += nr

            def conv_ksd(li, l):
                t, R, C = WB["w_ksd"]
                for kv in range(2):
                    for rpt in range(2):
                        src = AP(Dm["w_in"], l * D * IN_COLS + C_KS + kv * 64, [[IN_COLS, D], [1, 64]])
                        dst = AP(t, li * D * 256 + (2 * kv + rpt) * 64, [[256, D], [1, 64]])
                        cdma(dst, src, ("wb", "w_ksd", li), f"cv{li}w_ksd")

            def conv_layer(li, l):
                conv("w_in", li, l)
                conv_ksd(li, l)
                for name in ("w_pa", "w_pb", "w_o", "w_ffn_in", "w_ffn_out"):
                    conv(name, li, l)

            conv_layer(0, LAY[0])

            stop_at(1)
            A_act(sm("cneg", 0, 8), sm("cact", 0, 8), AF.Exp, [("cact",)], [("cneg",)], scale=-1.0)
            V_ts(sm("cneg", 0, 8), sm("cneg", 0, 8), 1.0, None, ALU.add, None, [("cneg",)], [("cneg",)])
            V_recip(sm("cneg", 0, 8), sm("cneg", 0, 8), [("cneg",)], [("cneg",)])
            V_tt(sm("cact", 0, 8), sm("cact", 0, 8), sm("cneg", 0, 8), ALU.mult, [("cact",), ("cneg",)], [("cact",)])

            mm(PSB[0][0:8, 0:384], relb[:, 0:8], oht[:, 0, :], True, True,
               [("relb",), ("oht",)], [("ps", 0)])
            V_copy(Fsb[0:8, :], PSB[0][0:8, 0:384], [("ps", 0)], [("Fsb", 0)])
            mm(PSB[1][0:16, 0:384], relb[:, 8:24], oht[:, 1, :], True, True,
               [("relb",), ("oht",)], [("ps", 1)])
            V_copy(tmp[0:16, 0, 0:384], PSB[1][0:16, 0:384], [("ps", 1)], [("tmp", 0)])
            dma("sp", AP(Fd, 0, [[384, 8], [1, 384]]), Fsb[0:8, :], [("Fsb", 0)], [("Fd",)], "fd")
            dma("sp", AP(Fd, 8 * 384, [[384, 16], [1, 384]]), tmp[0:16, 0, 0:384], [("tmp", 0)], [("Fd",)], "fd")
            _tmpi[0] = 1
            for hp in range(12):
                ti = newtmp()
                Hs = tmp[:, ti, :]
                for k in range(2):
                    h = 2 * hp + k
                    dma("sp", Hs[:, k * 256:(k + 1) * 256], AP(Fd, h * 384, [[1, 128], [1, 256]]),
                        [("Fd",)], [("tmp", ti)], f"hk{ti}")
                b = hp % 2
                mm(PSB[b][:, :], Jt[:], Hs, True, True, [("Jt",), ("tmp", ti)], [("ps", b)])
                V_copy(Tt[:, 2 * hp:2 * hp + 2, :].rearrange("p a b -> p (a b)"), PSB[b][:, :],
                       [("ps", b)], [("Tt", 2 * hp), ("Tt", 2 * hp + 1)])

            stop_at(2)
            bankrot = [0]

            def nbank(pool):
                b = pool[bankrot[0] % len(pool)]
                bankrot[0] += 1
                return b

            for li, l in enumerate(LAY):
                lam_init = 0.8 - 0.6 * math.exp(-0.3 * (l + cfg.lam_layer))
                dve(lambda e: e.memset(carry[:], 0.0), [], [("carry", j) for j in range(NFF)])
                dma("sp", sm("adab", 0, 48), AP(Dm["adabT"], l * 128 * 48, [[48, 128], [1, 48]]), [], [("adab",)], "lp")
                dma("sp", sm("normg", 0, 16), AP(Dm["normg"], l * 128 * 16, [[16, 128], [1, 16]]), [], [("normg",)], "lp")
                dma("sp", lamt[:], AP(Dm["lamv"], l * 128 * 256, [[256, 128], [1, 256]]), [], [("lamt",)], "lp")
                dma("sp", sm("subg"), AP(Dm["sublnT"], l * 128, [[1, 128], [1, 1]]), [], [("subg",)], "lp")
                dma("sp", sm("sinks", 0, 8), AP(Dm["sinksT"], l * 128 * 8, [[8, 128], [1, 8]]), [], [("sinks",)], "lp")
                dma("sp", convp[:], AP(Dm["convp"], l * 128 * NFF * 4, [[NFF * 4, 128], [1, NFF * 4]]), [], [("convp",)], "lp")
                dma("sp", adabrow[:, 0, :], AP(Dm["ada_b"], l * 6 * D + 2 * D, [[D, 1], [1, D]]), [], [("xg", 0), ("xg", 1)], "lp")
                dma("sp", adabrow[:, 1, :], AP(Dm["ada_b"], l * 6 * D + 5 * D, [[D, 1], [1, D]]), [], [("xg", 0), ("xg", 1)], "lp")
                lt = lamt[:].rearrange("p (a b) -> p a b", a=4)
                V_tt(lamt[:, 0:64], lamt[:, 0:64], lamt[:, 64:128], ALU.mult, [("lamt",)], [("lamt",)])
                V_tt(lamt[:, 128:192], lamt[:, 128:192], lamt[:, 192:256], ALU.mult, [("lamt",)], [("lamt",)])
                dve(lambda e: e.reduce_sum(out=sm("lam4", 0, 1), in_=lamt[:, 0:64], axis=mybir.AxisListType.X),
                    [("lamt",)], [("lam4",)])
                dve(lambda e: e.reduce_sum(out=sm("lam4", 1, 1), in_=lamt[:, 128:192], axis=mybir.AxisListType.X),
                    [("lamt",)], [("lam4",)])
                A_act(sm("lam4", 2, 2), sm("lam4", 0, 2), AF.Exp, [("lam4",)], [("lam4",)])
                V_tt(sm("nlam"), sm("lam4", 3, 1), sm("lam4", 2, 1), ALU.subtract, [("lam4",)], [("nlam",)])
                V_ts(sm("nlam"), sm("nlam"), -lam_init, None, ALU.add, None, [("nlam",)], [("nlam",)])
                V_ts(sm("sg"), sm("subg"), 1.0 - lam_init, None, ALU.mult, None, [("subg",)], [("sg",)])
                A_act(sm("esink", 0, 8), sm("sinks", 0, 8), AF.Exp, [("sinks",)], [("esink",)])

                cbc = tmp[:, 4:6, :].rearrange("p a (b c) -> p (a b) c", c=128)
                for kc in range(8):
                    V_ts(cbc[:, kc, :], ones_f[:], sm("cact", kc, 1), None, ALU.mult, None,
                         [("ones_f",), ("cact",)], [("tmp", 4), ("tmp", 5)])
                MB = 7
                for sl in range(24):
                    c0 = sl * 256
                    slot = wget(("ada", l, c0))
                    wv = wbuf[:, slot, :, :].bitcast(F32)
                    sec = c0 // D
                    if sec in (2, 5):
                        which = 0 if sec == 2 else 1
                        cc0 = c0 - sec * D
                        b = nbank([4, 5])
                        for kc in range(8):
                            mm(PSB[b][:, 0:256], cbc[:, kc, :], wv[:, kc, :], kc == 0, False,
                               [("tmp", 4), ("tmp", 5), ("w", slot)], [("ps", b)])
                        mm(PSB[b][:, 0:256], ones_f[0:1, :], adabrow[0:1, which, cc0:cc0 + 256], False, True,
                           [("ones_f",), ("xg", 0), ("xg", 1)], [("ps", b)])
                        V_copy(gbc[:, which, cc0:cc0 + 256], PSB[b][:, 0:256], [("ps", b)], [("gbc", which)])
                    else:
                        mcol0 = {0: 0, 1: 8, 3: 16, 4: 24}[sec] + (c0 - sec * D) // 128
                        for ci in range(2):
                            for kc in range(8):
                                mm(PSB[MB][:, mcol0 + ci:mcol0 + ci + 1], wv[:, kc, ci * 128:(ci + 1) * 128],
                                   sm("cact", kc, 1), kc == 0, kc == 7,
                                   [("w", slot), ("cact",)], [("ps", MB)])
                adv = small[:, SM["adab"][0]:SM["adab"][0] + 48]
                for (mo, ao) in ((0, 0), (8, 8), (16, 24), (24, 32)):
                    V_tt(sm("modT", mo, 8), PSB[MB][:, mo:mo + 8], adv[:, ao:ao + 8], ALU.add,
                         [("ps", MB), ("adab",)], [("modT",)])
                V_stt(sm("A1", 0, 8), sm("modT", 8, 8), 1.0, sm("normg", 0, 8), ALU.add, ALU.mult,
                      [("modT",), ("normg",)], [("A1",)])
                V_stt(sm("A2", 0, 8), sm("modT", 24, 8), 1.0, sm("normg", 8, 8), ALU.add, ALU.mult,
                      [("modT",), ("normg",)], [("A2",)])
                for nm_, an_, bo_ in (("AB1", "A1", 0), ("AB2", "A2", 16)):
                    abv = sm(nm_, 0, 16).rearrange("p (k t) -> p k t", t=2)
                    V_copy(abv[:, :, 0], sm(an_, 0, 8), [(an_,)], [(nm_,)])
                    V_copy(abv[:, :, 1], sm("modT", bo_, 8), [("modT",), (nm_,)], [(nm_,)])

                stop_at(3)
                def norm_stage(Aname, Boff):
                    for t in range(4):
                        xt = xg[:, t, :]
                        yb = yf[:, t % 2, :]
                        yr = ("yf", t % 2)
                        stop_at(3.1)
                        A_act(yb, xt, AF.Square, [("xg", t)], [yr, ("ss", t)], accum=sm("ss", t, 1))
                        stop_at(3.2)
                        A_act(sm("lnv", t, 1), sm("ss", t, 1), AF.Ln, [("ss", t), ("eps",)], [("lnv", t)],
                              scale=1.0 / D, bias=sm("eps"))
                        A_act(sm("rstd", t, 1), sm("lnv", t, 1), AF.Exp, [("lnv", t)], [("rstd", t)], scale=-0.5)
                        stop_at(3.3)
                        V_ts(yb, xt, sm("rstd", t, 1), None, ALU.mult, None, [("xg", t), ("rstd", t)], [yr])
                        stop_at(3.4)
                        bb = (4, 5) if t % 2 == 0 else (6, 7)
                        for kc in range(8):
                            b = bb[kc // 4]
                            pe(lambda e, kc=kc, b=b, yb=yb: e.transpose(
                                out=PSB[b][:, (kc % 4) * 128:(kc % 4 + 1) * 128], in_=yb[:, kc * 128:(kc + 1) * 128],
                                identity=identf[:]),
                               [yr, ("identf",)], [("ps", b)])
                        stop_at(3.5)
                        for kc in range(8):
                            b = bb[kc // 4]
                            o_ = hT[:, kc, t * 128:(t + 1) * 128]
                            i_ = PSB[b][:, (kc % 4) * 128:(kc % 4 + 1) * 128]
                            if True:
                                A_act(o_, i_, AF.Identity, [("ps", b), (Aname,)], [("hT", kc, t)],
                                      scale=sm(Aname, 2 * kc, 1), bias=sm(Aname, 2 * kc + 1, 1))
                            else:
                                V_ts(o_, i_, sm(Aname, 2 * kc, 1), sm(Aname, 2 * kc + 1, 1), ALU.mult, ALU.add,
                                     [("ps", b), (Aname,)], [("hT", kc, t)])
                            stop_at(3.51 + kc * 0.01)
                        stop_at(3.6 + t * 0.01)

                def fm_proj(slot, ci, bank, rhs3, rres, nk=8):
                    for kc in range(nk):
                        mm(PSB[bank][:, :], wbuf[:, slot, kc, ci * 128:(ci + 1) * 128], rhs3[:, kc, :],
                           kc == 0, kc == nk - 1, [("w", slot)] + rres(kc), [("ps", bank)])

                hres = lambda kc: [("hT", kc, t_) for t_ in range(4)]
                evac_i = [0]

                def evac(out, in_, reads, writes):
                    evac_i[0] += 1
                    if evac_i[0] % 2:
                        A_act(out, in_, AF.Copy, reads, writes)
                    else:
                        V_copy(out, in_, reads, writes)

                def kst(h):
                    return (kstage[:, h, :], ("big", 16 + h)) if h < 6 else (kst2[:, h - 6, :], ("kst2", h - 6))

                for g in range(NG):
                    xsrc = Dm["x"] if li == 0 else out_t
                    xread = [] if li == 0 else [("xd", g * 4 + t) for t in range(4)]
                    for t in range(4):
                        dma("sp", xg[:, t, :], AP(xsrc, (g * 4 + t) * 128 * D, [[D, 128], [1, D]]),
                            [("xd", g * 4 + t)] if li else [], [("xg", t)], f"xg{t}")
                    norm_stage("AB1", 0)
                    stop_at(4)
                    gs = g % 2
                    DP = [0, 1, 2, 3, 4, 5]
                    for s2 in range(2):
                        slot = wget(("bf", "w_in", li, 0, 8, C_VA + s2 * 512, 512))
                        for t in range(4):
                            b = nbank(DP)
                            for kc in range(8):
                                mm(PSB[b][:, :], hT[:, kc, t * 128:(t + 1) * 128], wbuf[:, slot, kc, :],
                                   kc == 0, kc == 7, [("w", slot), ("hT", kc, t)], [("ps", b)])
                            ov = vstage[:, s2 * 4:s2 * 4 + 4, t * 128:(t + 1) * 128]
                            iv = PSB[b][:, :].rearrange("p (a b) -> p a b", a=4)
                            evac(ov, iv, [("ps", b)], [("vm", s2 * 4 + k) for k in range(4)])
                    dma("pool", AP(Vc, g * 4 * 128, [[NBLK * 128, 128], [128 * NBLK * 128, 8], [1, GT]]), vstage[:],
                        [("vm", k) for k in range(8)], [("Vc", g)], "vst")
                    if li + 1 < nlay:
                        if g == 0:
                            defer_on[0] = True
                            conv_layer(li + 1, LAY[li + 1])
                            defer_on[0] = False
                        ngr = max(1, NG - 1)
                        per = (len(deferred) + ngr - 1) // ngr if g == 0 else per_saved[0]
                        if g == 0:
                            per_saved[0] = per
                        for _ in range(per if g < NG - 1 else len(deferred)):
                            if not deferred:
                                break
                            dst_, src_, res_, key_ = deferred.pop(0)
                            dma("pool", dst_, src_, [], [res_], key_)
                    slot = wget(("bf", "w_in", li, 0, 8, C_VS, 128))
                    for t in range(4):
                        b = nbank(DP)
                        for kc in range(8):
                            mm(PSB[b][:, 0:128], hT[:, kc, t * 128:(t + 1) * 128], wbuf[:, slot, kc, 0:128],
                               kc == 0, kc == 7, [("w", slot), ("hT", kc, t)], [("ps", b)])
                        evac(vsw[:, gs, t, :], PSB[b][:, 0:128], [("ps", b)], [("vsw", gs, t)])

                    for s2 in range(2):
                        slot = wget(("bf", "w_in", li, 0, 8, C_QA + s2 * 512, 512))
                        for ci in range(4):
                            b = nbank(DP)
                            fm_proj(slot, ci, b, hT, hres)
                            oc = s2 * 4 + ci
                            evac(qT[:, oc, :], PSB[b][:, :], [("ps", b)], [("big", oc)])
                    for s2 in range(2):
                        slot = wget(("bf", "w_in", li, 0, 8, C_KA + s2 * 512, 512))
                        for ci in range(4):
                            b = nbank(DP)
                            fm_proj(slot, ci, b, hT, hres)
                            ko, kr = kst(s2 * 4 + ci)
                            evac(ko, PSB[b][:, :], [("ps", b)], [kr])
                    dma("pool", AP(Kc, g * GT, [[S, 128], [128 * S, 6], [1, GT]]), kstage,
                        [("big", 16 + h) for h in range(6)], [("Kc", g)], "kst")
                    dma("pool", AP(Kc, 6 * 128 * S + g * GT, [[S, 128], [128 * S, 2], [1, GT]]), kst2[:],
                        [("kst2", 0), ("kst2", 1)], [("Kc", g)], "kst")
                    for s2 in range(2):
                        slot = wget(("bf", "w_in", li, 0, 8, C_QS + s2 * 512, 512))
                        for ci in range(4):
                            b = nbank(DP)
                            fm_proj(slot, ci, b, hT, hres)
                            oc = s2 * 4 + ci
                            evac(qsT[:, oc, :], PSB[b][:, :], [("ps", b)], [("big", 8 + oc)])
                    slot = wget(("bf", "w_ksd", li, 0, 8, 0, 256))
                    for kv in range(2):
                        b = nbank(DP)
                        fm_proj(slot, kv, b, hT, hres)
                        evac(ksw[:, gs, kv, :], PSB[b][:, :], [("ps", b)], [("ksw", gs, kv)])
                    stop_at(5)
                    pend = []
                    SKEW = 3

                    def push(fn):
                        pend.append(fn)
                        while len(pend) > SKEW:
                            pend.pop(0)()

                    def flush():
                        while pend:
                            pend.pop(0)()

                    accsel = [0]
                    SB_ = [0, 1, 2, 7]
                    srot = [0]

                    for ch in range(8):
                        ab = accsel[0] % 2
                        accsel[0] += 1
                        OB, UB = 3 + 2 * ab, 4 + 2 * ab
                        for half in range(2):
                            j = 2 * ch + half
                            kv = j // 8
                            p0 = half * 64
                            first = [True]
                            for jj in range(4 * g - 1, 4 * g + 4):
                                if jj < 0:
                                    continue
                                qb0, qb1 = max(jj, 4 * g), min(jj + 1, 4 * g + 3)
                                c0, c1 = (qb0 - 4 * g) * 128, (qb1 - 4 * g + 1) * 128
                                N = c1 - c0
                                if jj == 4 * g - 1:
                                    ks_, kc0, vs_, vt = 1 - gs, 384, 1 - gs, 3
                                else:
                                    ks_, kc0, vs_, vt = gs, (jj - 4 * g) * 128, gs, jj - 4 * g
                                tc0 = 0 if qb0 == jj else 128
                                sbk = SB_[srot[0] % 4]
                                srot[0] += 1
                                mm(PSB[sbk][:, 0:N], ksw[p0:p0 + 64, ks_, kv, kc0:kc0 + 128],
                                   qsT[p0:p0 + 64, ch, c0:c1], True, True,
                                   [("ksw", ks_, kv), ("big", 8 + ch)], [("ps", sbk)])
                                ti = newtmp()
                                V_stt(tmp[:, ti, 0:N], PSB[sbk][:, 0:N], SCALE, Tt[:, 8 + j, tc0:tc0 + N],
                                      ALU.mult, ALU.add, [("ps", sbk), ("Tt", 8 + j)], [("tmp", ti)])
                                pi = newpt()
                                A_act(pT[:, pi, 0:N], tmp[:, ti, 0:N], AF.Exp, [("tmp", ti)], [("pT", pi)])

                                def pv(pi=pi, N=N, c0=c0, c1=c1, vs_=vs_, vt=vt, kv=kv, p0=p0, st=first[0],
                                       OB=OB, UB=UB):
                                    mm(PSB[OB][p0:p0 + 64, c0:c1], vsw[:, vs_, vt, kv * 64:(kv + 1) * 64],
                                       pT[:, pi, 0:N], st, False, [("vsw", vs_, vt), ("pT", pi)], [("ps", OB)], skip=True)
                                    mm(PSB[UB][p0:p0 + 64, c0:c1], ones_bf[:, 0:64],
                                       pT[:, pi, 0:N], st, False, [("ones_bf",), ("pT", pi)], [("ps", UB)], skip=True)
                                push(pv)
                                first[0] = False

                        def fin_sw(ch=ch, OB=OB, UB=UB):
                            ti = newtmp()
                            V_ts(tmp[:, ti, :], PSB[UB][:, :], sm("esink", ch, 1), None, ALU.add, None,
                                 [("ps", UB), ("esink",)], [("tmp", ti)])
                            V_recip(tmp[:, ti, :], tmp[:, ti, :], [("tmp", ti)], [("tmp", ti)])
                            V_tt(ybT[:, ch, :], PSB[OB][:, :], tmp[:, ti, :], ALU.mult,
                                 [("ps", OB), ("tmp", ti)], [("ybT", ch)])
                        push(fin_sw)

                    flush()
                    stop_at(6)
                    nkb = 4 * (g + 1)
                    nch = (nkb + 7) // 8
                    for h in range(8):
                        cb = Tt[:, h, 255:256]
                        hold = [None]
                        for m in range(2):
                            ab = accsel[0] % 2
                            accsel[0] += 1
                            OB, UB = 3 + 2 * ab, 4 + 2 * ab
                            p0 = m * 64
                            for kb in range(nkb):
                                kk = kb % 8
                                if kk == 0:
                                    slot = kvget((li, g, h, m, kb // 8))
                                qb0 = max(kb, 4 * g)
                                c0 = (qb0 - 4 * g) * 128
                                N = GT - c0
                                sbk = SB_[srot[0] % 4]
                                srot[0] += 1
                                mm(PSB[sbk][:, 0:N], kbuf[p0:p0 + 64, slot, kk * 128:(kk + 1) * 128],
                                   qT[p0:p0 + 64, h, c0:GT], True, True,
                                   [("kb", slot), ("big", h)], [("ps", sbk)])
                                pi = newpt()
                                if kb < 4 * g - 1:
                                    A_act(pT[:, pi, 0:N], PSB[sbk][:, 0:N], AF.Exp, [("ps", sbk), ("Tt", h)],
                                          [("pT", pi)], scale=SCALE, bias=cb)
                                else:
                                    if kb == 4 * g - 1:
                                        nb, tc0 = 128, 128
                                    else:
                                        nb, tc0 = min(256, N), 0
                                    ti = newtmp()
                                    V_stt(tmp[:, ti, 0:nb], PSB[sbk][:, 0:nb], SCALE, Tt[:, h, tc0:tc0 + nb],
                                          ALU.mult, ALU.add, [("ps", sbk), ("Tt", h)], [("tmp", ti)])
                                    A_act(pT[:, pi, 0:nb], tmp[:, ti, 0:nb], AF.Exp, [("tmp", ti)], [("pT", pi)])
                                    if N > nb:
                                        A_act(pT[:, pi, nb:N], PSB[sbk][:, nb:N], AF.Exp, [("ps", sbk), ("Tt", h)],
                                              [("pT", pi)], scale=SCALE, bias=cb)

                                def pv(pi=pi, N=N, c0=c0, slot=slot, kk=kk, st=(kb == 0), OB=OB, UB=UB):
                                    mm(PSB[OB][:, c0:GT], vbuf[:, slot, kk, :], pT[:, pi, 0:N], st, False,
                                       [("vb", slot), ("pT", pi)], [("ps", OB)], skip=True)
                                    mm(PSB[UB][:, c0:GT], ones_bf[:], pT[:, pi, 0:N], st, False,
                                       [("ones_bf",), ("pT", pi)], [("ps", UB)], skip=True)
                                push(pv)

                            def fin_da(m=m, h=h, OB=OB, UB=UB, hold=hold):
                                tr = newtmp()
                                V_recip(tmp[:, tr, :], PSB[UB][:, :], [("ps", UB)], [("tmp", tr)])
                                o0 = o0buf[:, h % 2, :]
                                o0r = ("o0", h % 2)
                                if m == 0:
                                    V_tt(o0, PSB[OB][:, :], tmp[:, tr, :], ALU.mult,
                                         [("ps", OB), ("tmp", tr)], [o0r])
                                else:
                                    V_tt(tmp[:, tr, :], PSB[OB][:, :], tmp[:, tr, :], ALU.mult,
                                         [("ps", OB), ("tmp", tr)], [("tmp", tr)])
                                    V_stt(o0, tmp[:, tr, :], sm("nlam"), o0, ALU.mult, ALU.add,
                                          [("tmp", tr), o0r, ("nlam",)], [o0r])
                                    A_act(tmp[:, tr, :], o0, AF.Square, [o0r], [("tmp", tr)])
                                    stb = SB_[srot[0] % 4]
                                    srot[0] += 1
                                    mm(PSB[stb][:, :], ones_f[:], tmp[:, tr, :], True, True,
                                       [("ones_f",), ("tmp", tr)], [("ps", stb)])
                                    A_act(tmp[:, tr, :], PSB[stb][:, :], AF.Ln, [("ps", stb), ("eps",)], [("tmp", tr)],
                                          scale=1.0 / 128, bias=sm("eps"))
                                    A_act(tmp[:, tr, :], tmp[:, tr, :], AF.Exp, [("tmp", tr)], [("tmp", tr)], scale=-0.5)
                                    V_stt(yaT[:, h, :], o0, sm("sg"), tmp[:, tr, :], ALU.mult, ALU.mult,
                                          [o0r, ("tmp", tr), ("sg",)], [("yaT", h)])
                            push(fin_da)
                    flush()
                    if "att" in cfg.debug and g == cfg.NG - 1 and li == 0:
                        for nm_, t_, rs_ in (("dbg_yaT", yaT, [("yaT", k) for k in range(8)]),
                                             ("dbg_ybT", ybT, [("ybT", k) for k in range(8)]),
                                             ("dbg_hT", hT, [("hT", k, t_) for k in range(8) for t_ in range(4)])):
                            dt_ = dbg_out[nm_]
                            dma("pool", dt_.ap(), t_[:], rs_, [("dbg", nm_)], "dbg")

                    stop_at(7)
                    for s2 in range(2):
                        slot = wget(("bf", "w_pa", li, 0, 8, s2 * 512, 512))
                        for ci in range(4):
                            fm_proj(slot, ci, ci, yaT, lambda kc: [("yaT", kc)])
                        slot = wget(("bf", "w_in", li, 0, 8, C_GA + s2 * 512, 512))
                        for ci in range(4):
                            fm_proj(slot, ci, 4 + ci, hT, hres)
                        for ci in range(4):
                            A_act(tmp[:, ci, :], PSB[4 + ci][:, :], AF.Sigmoid, [("ps", 4 + ci)], [("tmp", ci)])
                            V_tt(tmp[:, ci, :], PSB[ci][:, :], tmp[:, ci, :], ALU.mult,
                                 [("ps", ci), ("tmp", ci)], [("tmp", ci)])
                        slot = wget(("bf", "w_pb", li, 0, 8, s2 * 512, 512))
                        for ci in range(4):
                            fm_proj(slot, ci, ci, ybT, lambda kc: [("ybT", kc)])
                        slot = wget(("bf", "w_in", li, 0, 8, C_GB + s2 * 512, 512))
                        for ci in range(4):
                            fm_proj(slot, ci, 4 + ci, hT, hres)
                        for ci in range(4):
                            tb = 4 + (ci % 2)
                            A_act(tmp[:, tb, :], PSB[4 + ci][:, :], AF.Sigmoid, [("ps", 4 + ci)], [("tmp", tb)])
                            V_tt(tmp[:, tb, :], PSB[ci][:, :], tmp[:, tb, :], ALU.mult,
                                 [("ps", ci), ("tmp", tb)], [("tmp", tb)])
                            V_tt(mergedT[:, s2 * 4 + ci, :], tmp[:, ci, :], tmp[:, tb, :], ALU.add,
                                 [("tmp", ci), ("tmp", tb)], [("vm", s2 * 4 + ci)])
                    dbgon = "att" in cfg.debug and g == cfg.NG - 1 and li == 0
                    if dbgon:
                        dma("pool", dbg_out["dbg_mg"].ap(), mergedT[:], [("vm", k) for k in range(8)], [("dbg", "mg")], "dbg")
                        dma("pool", dbg_out["dbg_gbc"].ap(), gbc[:], [("gbc", 0), ("gbc", 1)], [("dbg", "gbc")], "dbg")
                    AP8 = [0, 1, 2, 3, 4, 5, 6, 7]
                    for chh in range(2):
                        slot = wget(("bf", "w_o", li, 0, 8, chh * 512, 512))
                        for t in range(4):
                            b = nbank(AP8)
                            for cc in range(8):
                                mm(PSB[b][:, :], mergedT[:, cc, t * 128:(t + 1) * 128], wbuf[:, slot, cc, :],
                                   cc == 0, cc == 7, [("vm", cc), ("w", slot)], [("ps", b)])
                            ti = newtmp()
                            V_tt(tmp[:, ti, :], PSB[b][:, :], gbc[:, 0, chh * 512:(chh + 1) * 512], ALU.mult,
                                 [("ps", b), ("gbc", 0)], [("tmp", ti)])
                            xs = xg[:, t, chh * 512:(chh + 1) * 512]
                            V_tt(xs, xs, tmp[:, ti, :], ALU.add, [("xg", t), ("tmp", ti)], [("xg", t)])
                    stop_at(8)
                    if dbgon:
                        dma("pool", dbg_out["dbg_x1"].ap(), xg[:], [("xg", k) for k in range(4)], [("dbg", "x1")], "dbg")
                    norm_stage("AB2", 16)
                    slotA = slotB = None
                    for j in range(NFF):
                        s2, ci = j // 4, j % 4
                        ncol = 512 if s2 < 5 else 256
                        if ci == 0:
                            slotA = wget(("bf", "w_ffn_in", li, 0, 8, s2 * 512, ncol))
                            slotB = wget(("bf", "w_ffn_in", li, 0, 8, D_FF + s2 * 512, ncol), live=2)
                        bA, bB = [(0, 1), (2, 3), (4, 5)][j % 3]
                        fm_proj(slotA, ci, bA, hT, hres)
                        fm_proj(slotB, ci, bB, hT, hres)
                        ab_ = abuf[:, j % 2, :]
                        abr = ("abuf", j % 2)
                        P.op("pool", lambda e, ab_=ab_, j=j: e.tensor_copy(out=ab_[:, 0:2], in_=carry[:, j, :]),
                             [("carry", j)], [abr])
                        A_act(ab_[:, 2:2 + GT], PSB[bA][:, :], AF.Copy, [("ps", bA)], [abr])
                        P.op("pool", lambda e, ab_=ab_, j=j: e.tensor_copy(out=carry[:, j, :], in_=ab_[:, GT:GT + 2]),
                             [abr], [("carry", j)])
                        cw = lambda k, j=j: convp[:, j * 4 + k:j * 4 + k + 1]
                        ti = newtmp()
                        A_act(tmp[:, ti, :], ab_[:, 2:2 + GT], AF.Identity, [abr, ("convp",)], [("tmp", ti)],
                              scale=cw(2), bias=cw(3))
                        V_stt(tmp[:, ti, :], ab_[:, 1:1 + GT], cw(1), tmp[:, ti, :], ALU.mult, ALU.add,
                              [abr, ("convp",), ("tmp", ti)], [("tmp", ti)])
                        V_stt(tmp[:, ti, :], ab_[:, 0:GT], cw(0), tmp[:, ti, :], ALU.mult, ALU.add,
                              [abr, ("convp",), ("tmp", ti)], [("tmp", ti)])
                        A_act(tmp[:, ti, :], tmp[:, ti, :], AF.Silu, [("tmp", ti)], [("tmp", ti)])
                        V_tt(uT[:, j, :], PSB[bB][:, :], tmp[:, ti, :], ALU.mult,
                             [("ps", bB), ("tmp", ti)], [("big", j)])
                    last_layer = (li == nlay - 1)
                    if dbgon:
                        dma("pool", dbg_out["dbg_u"].ap(), uT[:], [("big", k) for k in range(22)], [("dbg", "u")], "dbg")
                    for chh in range(2):
                        accb = [0, 1, 2, 3] if chh == 0 else [4, 5, 6, 7]
                        for s3 in range(3):
                            j0 = s3 * 8
                            nj = min(8, NFF - j0)
                            slot = wget(("bf", "w_ffn_out", li, j0 * 128, nj, chh * 512, 512))
                            for t in range(4):
                                for jj in range(nj):
                                    j = j0 + jj
                                    mm(PSB[accb[t]][:, :], uT[:, j, t * 128:(t + 1) * 128], wbuf[:, slot, jj, :],
                                       j == 0, j == NFF - 1, [("big", j), ("w", slot)], [("ps", accb[t])])
                        for t in range(4):
                            ti = newtmp()
                            V_tt(tmp[:, ti, :], PSB[accb[t]][:, :], gbc[:, 1, chh * 512:(chh + 1) * 512], ALU.mult,
                                 [("ps", accb[t]), ("gbc", 1)], [("tmp", ti)])
                            xs = xg[:, t, chh * 512:(chh + 1) * 512]
                            V_tt(xs, xs, tmp[:, ti, :], ALU.add, [("xg", t), ("tmp", ti)], [("xg", t)])
                    if dbgon:
                        dma("pool", dbg_out["dbg_x2"].ap(), xg[:], [("xg", k) for k in range(4)], [("dbg", "x2")], "dbg")
                    for t in range(4):
                        dst = AP(out_t, (g * 4 + t) * 128 * D, [[D, 128], [1, D]])
                        if last_layer and cfg.final_norm:
                            xt = xg[:, t, :]
                            A_act(yf[:, t % 2, :], xt, AF.Square, [("xg", t)], [("yf", t % 2), ("ss2", t)], accum=sm("ss2", t, 1))
                            A_act(sm("lnv2", t, 1), sm("ss2", t, 1), AF.Ln, [("ss2", t), ("eps",)], [("lnv2", t)],
                                  scale=1.0 / D, bias=sm("eps"))
                            A_act(sm("rstd2", t, 1), sm("lnv2", t, 1), AF.Exp, [("lnv2", t)], [("rstd2", t)], scale=-0.5)
                            V_stt(xt, xt, sm("rstd2", t, 1), fgbt[:], ALU.mult, ALU.mult,
                                  [("xg", t), ("rstd2", t), ("fgb",)], [("xg", t)])
                        dma("pool", dst, xg[:, t, :], [("xg", t)], [("xd", g * 4 + t)], f"xs{t}")
            P.op("pool", None, [("xd", i) for i in range(NBLK)] + [("dbg", k) for k in dbg_out], [])

        fgbt = sb("fgbt", [128, D], F32)

        WS1, KS1 = Stream("w", NW), Stream("kv", 4)
        try:
            emit_all(_NullProg(), WS1, KS1)
        except StopEmit:
            pass
        P = Prog()
        P.op("sp", lambda e: e.dma_start(out=fgbt[:], in_=Dm["fgb"].ap()), [], [("fgb",)], key="c0")
        WS2, KS2 = Stream("w", NW, WS1.specs), Stream("kv", 4, KS1.specs)
        try:
            emit_all(P, WS2, KS2)
        except StopEmit:
            pass
        P.final_wait("pool")
        P.finalize()
        semnames = P.sem_names()
        sems = {}
        for i, sname in enumerate(semnames):
            sems[sname] = es.enter_context(nc.semaphore(f"s{i}"))
        block = es.enter_context(nc.Block())

        @block.tensor
        def _(e):
            P.emit_engine("pe", e, sems)

        @block.scalar
        def _(e):
            P.emit_engine("act", e, sems)

        @block.vector
        def _(e):
            P.emit_engine("dve", e, sems)

        @block.gpsimd
        def _(e):
            P.emit_engine("pool", e, sems)

        @block.sync
        def _(e):
            P.emit_engine("sp", e, sems)
    return nc, sorted(dbg_out.keys())


def _host_layouts(inp, nl):
    f = np.float32
    L = nl
    oh_da, oh_sw = _onehots()
    shared = {
        "rel_bias": np.ascontiguousarray(inp["rel_bias"], f),
        "oh_da": oh_da, "oh_sw": oh_sw,
        "Jm": np.ascontiguousarray(np.eye(128, dtype=f)[::-1]),
        "ident": np.eye(128, dtype=f),
        "ada_w": np.ascontiguousarray(inp["ada_w"], f),
        "ada_b": np.ascontiguousarray(inp["ada_b"], f),
        "adabT": np.ascontiguousarray(np.asarray(inp["ada_b"], f).reshape(L, 48, 128).transpose(0, 2, 1)),
        "normg": np.ascontiguousarray(np.concatenate(
            [np.asarray(inp["norm_mix_g"], f).reshape(L, 8, 128).transpose(0, 2, 1),
             np.asarray(inp["norm_ffn_g"], f).reshape(L, 8, 128).transpose(0, 2, 1)], axis=2)),
        "lamv": np.ascontiguousarray(np.broadcast_to(np.concatenate(
            [np.asarray(inp[k], f) for k in ("lam_q1", "lam_k1", "lam_q2", "lam_k2")], axis=1)[:, None, :],
            (L, 128, 256))),
        "sublnT": np.ascontiguousarray(np.asarray(inp["subln_g"], f).reshape(L, 128, 1)),
        "sinksT": np.ascontiguousarray(np.repeat(np.asarray(inp["sinks"], f).reshape(L, 8, 2).transpose(0, 2, 1),
                                                 64, axis=1)),
        "convp": np.ascontiguousarray(np.concatenate(
            [np.asarray(inp["conv_w"], f).reshape(L, 3, NFF, 128).transpose(0, 3, 2, 1),
             np.asarray(inp["conv_b"], f).reshape(L, NFF, 128).transpose(0, 2, 1)[..., None]], axis=3
        ).reshape(L, 128, NFF * 4)),
        "fgb": np.ascontiguousarray(np.broadcast_to(np.asarray(inp["final_g"], f)[None, :], (128, D))),
    }
    for k in ("w_in", "w_pa", "w_pb", "w_o", "w_ffn_in", "w_ffn_out"):
        shared[k] = np.ascontiguousarray(inp[k], f)
    return shared


def _core_inputs(shared, x_b, c_b):
    m = dict(shared)
    m["x"] = np.ascontiguousarray(x_b, np.float32)
    m["cT"] = np.ascontiguousarray(np.asarray(c_b, np.float32).reshape(8, 128).T)
    return m


_NC_CACHE = {}


def _get_nc(S, layers, final_norm, debug=()):
    key = (S, tuple(layers), final_norm, tuple(debug))
    if key not in _NC_CACHE:
        _NC_CACHE[key] = build_nc(Cfg(S=S, layers=layers, final_norm=final_norm, debug=debug))
    return _NC_CACHE[key]


MODE = "fused"


def kernel(**inputs):
    x = np.asarray(inputs["x"], np.float32)
    c = np.asarray(inputs["c"], np.float32)
    B, S, _ = x.shape
    shared = _host_layouts(inputs, DEPTH)
    cores = list(range(B))
    if MODE == "fused":
        nc, _ = build_nc(Cfg(S=S, layers=range(DEPTH), final_norm=True))
        in_maps = [_core_inputs(shared, x[b], c[b]) for b in range(B)]
        res = run_bass_kernel_spmd(nc, in_maps, core_ids=cores)
        return np.stack([np.asarray(r["out"]) for r in res.results], axis=0).astype(np.float32)
    cur = [x[b] for b in range(B)]
    per_layer = ("ada_w", "ada_b", "norm_mix_g", "norm_ffn_g", "w_in", "lam_q1", "lam_k1", "lam_q2", "lam_k2",
                 "subln_g", "sinks", "w_pa", "w_pb", "w_o", "w_ffn_in", "conv_w", "conv_b", "w_ffn_out")
    for l in range(DEPTH):
        sub = dict(inputs)
        for k in per_layer:
            sub[k] = np.asarray(inputs[k])[l:l + 1]
        shared_l = _host_layouts(sub, 1)
        cfg = Cfg(S=S, layers=(0,), final_norm=(l == DEPTH - 1), nl_total=1)
        cfg.lam_layer = l
        nc, _ = build_nc(cfg)
        in_maps = [_core_inputs(shared_l, cur[b], c[b]) for b in range(B)]
        res = run_bass_kernel_spmd(nc, in_maps, core_ids=cores)
        cur = [np.asarray(r["out"]) for r in res.results]
    return np.stack(cur, axis=0).astype(np.float32)
```
